# Optimizing a Trainium2 kernel written in Bass

```python
import jax, jax.numpy as jnp
from jax import lax
import numpy as np

D_MODEL = 2048
BATCH = 4
SEQ = 2048
DEPTH = 4
DEC_BATCH = 128
DEC_SEQ = 4
PAST_LEN = 16384
PAGE_SIZE = 128

D_MIX = D_MODEL
W_A = D_MIX // 2
W_B = D_MIX - W_A
HGRN_EXPAND = 128
H_A = W_A // HGRN_EXPAND
DK = HGRN_EXPAND
DV = W_A // H_A
CONV_W = 3
N_META = 16
CHUNK = 64
EPS = 1e-6
F_FLOOR = 1e-30
SPLITS = (W_A, W_A, W_A, W_A, W_B, W_B, W_B, W_B)
D_IN = sum(SPLITS)

kernel_name = "hymba_hgrn2_shortconv_decode_step"


def rmsnorm(x, w):
    xf = x.astype(jnp.float32)
    y = xf * lax.rsqrt(jnp.mean(xf * xf, axis=-1, keepdims=True) + EPS)
    return (y * w.astype(jnp.float32)).astype(x.dtype)


def hgrn_chunk(S, inp):
    q, k, g, v = inp
    L = q.shape[1]
    G = jnp.cumsum(g, axis=1)
    causal = jnp.tril(jnp.ones((L, L), dtype=bool))[None, :, :, None, None]
    diff = G[:, :, None] - G[:, None, :]
    decay = jnp.where(causal, jnp.exp(jnp.minimum(diff, 0.0)), 0.0)
    A = jnp.sum(q[:, :, None] * k[:, None] * decay, axis=-1)
    o = (jnp.einsum('btsh,bshv->bthv', A, v)
         + jnp.einsum('bthk,bhkv->bthv', q * jnp.exp(G), S))
    G_last = G[:, -1]
    S_new = (jnp.exp(G_last)[..., None] * S
             + jnp.einsum('bshk,bshv->bhkv', k * jnp.exp(G_last[:, None] - G), v))
    return S_new, o


def hgrn_prompt(q, k, g, v):
    Bn = q.shape[0]
    S0 = jnp.zeros((Bn, H_A, DK, DV), jnp.float32)
    S, o_meta = hgrn_chunk(S0, (q[:, :N_META], k[:, :N_META], g[:, :N_META], v[:, :N_META]))

    def to_chunks(a):
        r = a[:, N_META:]
        n = r.shape[1] // CHUNK
        return r.reshape(Bn, n, CHUNK, *r.shape[2:]).swapaxes(0, 1)

    S, o_rest = lax.scan(hgrn_chunk, S, (to_chunks(q), to_chunks(k), to_chunks(g), to_chunks(v)))
    o_rest = o_rest.swapaxes(0, 1).reshape(Bn, -1, H_A, DV)
    return jnp.concatenate([o_meta, o_rest], axis=1), S


def short_conv(u, buf, w):
    T = u.shape[1]
    pad = jnp.concatenate([buf.astype(u.dtype), u], axis=1)
    y = sum(w[j] * pad[:, j:j + T] for j in range(CONV_W))
    return y, pad[:, -(CONV_W - 1):]


def mixer_layer(h, l, S0, buf, prompt, w_in, conv_w, lb_param, hgrn_norm_w, w_out,
                pre_norm_w, post_norm_w):
    Bn, T, _ = h.shape
    hn = rmsnorm(h, pre_norm_w[l])
    proj = jnp.einsum('btd,de->bte', hn, w_in[l]).astype(jnp.float32)
    q, fz, i, za, bg, cg, xt, zb = jnp.split(proj, list(np.cumsum(SPLITS)[:-1]), axis=-1)

    lb_w = jax.nn.softmax(lb_param.astype(jnp.float32), axis=0)
    lb_all = jnp.cumsum(lb_w, axis=0) - lb_w[0]
    lb = lb_all[l].reshape(H_A, DK)
    fz = fz.reshape(Bn, T, H_A, DK)
    f = lb + (1.0 - lb) * jax.nn.sigmoid(fz)
    log_f = jnp.log(jnp.maximum(f, F_FLOOR))
    k = (1.0 - lb) * jax.nn.sigmoid(-fz)
    q = q.reshape(Bn, T, H_A, DK)
    v = i.reshape(Bn, T, H_A, DV)
    if prompt:
        o, S = hgrn_prompt(q, k, log_f, v)
    else:
        S, o = hgrn_chunk(S0.astype(jnp.float32), (q, k, log_f, v))
    o = rmsnorm(o, hgrn_norm_w[l]).reshape(Bn, T, W_A) * jax.nn.silu(za)

    if prompt:
        buf = jnp.zeros((Bn, CONV_W - 1, W_B), jnp.float32)
    c, new_buf = short_conv(cg * xt, buf, conv_w[l].astype(jnp.float32))
    yb = bg * c * jax.nn.silu(zb)

    mix = jnp.einsum('bte,ed->btd', jnp.concatenate([o, yb], axis=-1).astype(h.dtype), w_out[l])
    h = h + rmsnorm(mix, post_norm_w[l])
    return h, S, new_buf


def setup_inputs(seed: int = 0) -> dict:
    key = jax.random.key(seed)
    ks = jax.random.split(key, 12)
    f32 = jnp.float32
    return {
        "x_prompt": jax.random.normal(ks[0], (BATCH, SEQ, D_MODEL), f32),
        "x_sample": jax.random.normal(ks[1], (DEC_BATCH, DEC_SEQ, D_MODEL), f32),
        "state_hgrn": 0.5 * jax.random.normal(ks[2], (DEPTH, DEC_BATCH, H_A, DK, DV), f32),
        "state_conv": jax.random.normal(ks[3], (DEPTH, DEC_BATCH, CONV_W - 1, W_B), f32),
        "meta_tokens": jax.random.normal(ks[4], (N_META, D_MODEL), f32),
        "w_in": jax.random.normal(ks[5], (DEPTH, D_MODEL, D_IN), f32) * D_MODEL ** -0.5,
        "conv_w": jax.random.normal(ks[6], (DEPTH, CONV_W, W_B), f32) * CONV_W ** -0.5,
        "lb_param": 0.5 * jax.random.normal(ks[7], (DEPTH, H_A * DK), f32),
        "hgrn_norm_w": 1.0 + 0.02 * jax.random.normal(ks[8], (DEPTH, DV), f32),
        "w_out": jax.random.normal(ks[9], (DEPTH, D_MIX, D_MODEL), f32) * D_MIX ** -0.5,
        "pre_norm_w": 1.0 + 0.02 * jax.random.normal(ks[10], (DEPTH, D_MODEL), f32),
        "post_norm_w": 1.0 + 0.02 * jax.random.normal(ks[11], (DEPTH, D_MODEL), f32),
    }


def reference(x_prompt, x_sample, state_hgrn, state_conv, meta_tokens, w_in, conv_w, lb_param,
              hgrn_norm_w, w_out, pre_norm_w, post_norm_w):
    Bp = x_prompt.shape[0]
    meta = jnp.broadcast_to(meta_tokens[None].astype(x_prompt.dtype), (Bp, N_META, D_MODEL))
    hp = jnp.concatenate([meta, x_prompt], axis=1)
    hs = x_sample
    Sp_list, bp_list, Ss_list, bs_list = [], [], [], []
    for l in range(DEPTH):
        hp, Sp, bp = mixer_layer(hp, l, None, None, True, w_in, conv_w, lb_param, hgrn_norm_w,
                                 w_out, pre_norm_w, post_norm_w)
        hs, Ss, bs = mixer_layer(hs, l, state_hgrn[l], state_conv[l], False, w_in, conv_w,
                                 lb_param, hgrn_norm_w, w_out, pre_norm_w, post_norm_w)
        Sp_list.append(Sp.astype(x_prompt.dtype))
        bp_list.append(bp.astype(x_prompt.dtype))
        Ss_list.append(Ss.astype(state_hgrn.dtype))
        bs_list.append(bs.astype(state_conv.dtype))
    y_prompt = hp[:, N_META:]
    y_sample = hs
    new_hgrn_prompt = jnp.stack(Sp_list)
    new_conv_prompt = jnp.stack(bp_list)
    new_hgrn_sample = jnp.stack(Ss_list)
    new_conv_sample = jnp.stack(bs_list)
    return (y_prompt, y_sample, new_hgrn_prompt, new_conv_prompt, new_hgrn_sample, new_conv_sample)
```

```python
from contextlib import ExitStack

import numpy as np
import concourse.bass as bass
import concourse.mybir as mybir
from concourse.bass_utils import run_bass_kernel_spmd

F32 = mybir.dt.float32
BF16 = mybir.dt.bfloat16
AF = mybir.ActivationFunctionType
ALU = mybir.AluOpType

D = 2048
DC = 16
NH = 8
EPS = 1e-6
N_CORES = 8


class Prog:
    ENG = ['pe', 'act', 'dve', 'pool', 'sp']

    def __init__(s, nc):
        s.nc = nc
        s.ops = []
        s.tok_w = {}
        s.tok_r = {}
        s.epoch = 0
        s.dma_cnt = {}
        s.bar = None
        s.bar_kw = None

    def barrier(s):
        last = {}
        for j, o in enumerate(s.ops):
            if o['dsem'] is not None:
                last[('dma', o['dsem'])] = j
            else:
                last[o['eng']] = j
        o = dict(eng='dve', meth='memset', kw=s.bar_kw, deps=set(last.values()), dsem=None, epoch=s.epoch)
        s.ops.append(o)
        s.bar = len(s.ops) - 1

    def op(s, eng, meth, kw, r=(), w=(), dsem=None):
        i = len(s.ops)
        deps = set()
        for t in r:
            if t in s.tok_w:
                deps.add(s.tok_w[t])
        for t in w:
            if t in s.tok_w:
                deps.add(s.tok_w[t])
            for x in s.tok_r.get(t, ()):
                deps.add(x)
        if s.bar is not None:
            deps.add(s.bar)
        o = dict(eng=eng, meth=meth, kw=kw, deps=deps, dsem=dsem, epoch=s.epoch)
        if dsem is not None:
            s.dma_cnt[dsem] = s.dma_cnt.get(dsem, 0) + 16
            o['dval'] = s.dma_cnt[dsem]
        s.ops.append(o)
        for t in r:
            s.tok_r.setdefault(t, []).append(i)
        for t in w:
            s.tok_w[t] = i
            s.tok_r[t] = []
        return i

    def emit(s):
        nc = s.nc
        needed = set()
        for o in s.ops:
            for d in o['deps']:
                p = s.ops[d]
                if p['dsem'] is None:
                    if p['eng'] == 'pe' and o['eng'] == 'pe':
                        continue
                    needed.add(d)
        cnt = {}
        for i, o in enumerate(s.ops):
            if o['dsem'] is None and i in needed:
                k = (o['eng'], o['epoch'])
                cnt[k] = cnt.get(k, 0) + 1
                o['seq'] = cnt[k]
        with ExitStack() as st:
            sems = {}
            for k in cnt:
                sems[k] = st.enter_context(nc.semaphore("s_%s_%d" % k))
            for k in s.dma_cnt:
                sems[('dma', k)] = st.enter_context(nc.semaphore("d_%s" % str(k)))
            block = st.enter_context(nc.Block())
            engobj = {'pe': block.tensor, 'act': block.scalar, 'dve': block.vector,
                      'pool': block.gpsimd, 'sp': block.sync}

            def body(ename):
                def f(e):
                    waited = {}
                    for o in s.ops:
                        if o['eng'] != ename:
                            continue
                        ws = {}
                        for d in o['deps']:
                            p = s.ops[d]
                            if p['dsem'] is not None:
                                key = ('dma', p['dsem'])
                                val = p['dval']
                            else:
                                if p['eng'] == 'pe' and ename == 'pe':
                                    continue
                                key = (p['eng'], p['epoch'])
                                val = p['seq']
                            if ws.get(key, 0) < val:
                                ws[key] = val
                        for key, val in ws.items():
                            if waited.get(key, 0) >= val:
                                continue
                            e.wait_ge(sems[key], val)
                            waited[key] = val
                        ins = getattr(e, o['meth'])(**o['kw'])
                        if o['dsem'] is not None:
                            ins.then_inc(sems[('dma', o['dsem'])], 16)
                        elif 'seq' in o:
                            ins.then_inc(sems[(o['eng'], o['epoch'])], 1)
                    if ename == 'sp':
                        for k, v in s.dma_cnt.items():
                            e.wait_ge(sems[('dma', k)], v)
                return f
            for ename in s.ENG:
                engobj[ename](body(ename))


def split_pieces(T):
    n = (T + 511) // 512
    base = ((T + n - 1) // n + 15) // 16 * 16
    out = []
    c = 0
    while c < T:
        out.append((c, min(T, c + base)))
        c += base
    return out


def build_nc(NL=4, NPCH=16, NSEQ=16, NW=3, CHAIN_RATIO=3, CH=128, CHAIN_DELAY=4):
    assert NSEQ % 4 == 0 and 4 * NSEQ <= 64
    NPT = 2 * NPCH * 64
    LS = 4 * NSEQ
    T0 = 16 + 64 * NPCH
    T1 = 64 * NPCH + LS
    TM = max(T0, T1)
    NPC = NPCH * 64 // CH
    NCHM = NPC + 1
    NCOL = NCHM + NSEQ

    nc = bass.Bass("TRN2", target_bir_lowering=False)
    dt = lambda n, s, k: nc.dram_tensor(n, s, F32, kind=k).ap()
    xp = dt("xp", [NPT, D], "ExternalInput")
    meta = dt("meta", [16, D], "ExternalInput")
    xs = dt("xs", [LS, D], "ExternalInput")
    sh = dt("sh", [NL, NSEQ, NH, 128, 128], "ExternalInput")
    sc = dt("sc", [NL, NSEQ * 2, 1024], "ExternalInput")
    w_in = dt("w_in", [NL, D, 8192], "ExternalInput")
    w_out = dt("w_out", [NL, D, D], "ExternalInput")
    conv_w = dt("conv_w", [NL * 24, 128], "ExternalInput")
    lb_param = dt("lb_param", [NL * 8, 128], "ExternalInput")
    hnw = dt("hnw", [NL, 128], "ExternalInput")
    prew = dt("prew", [NL * 16, 128], "ExternalInput")
    postw = dt("postw", [NL * 16, 128], "ExternalInput")
    yp = dt("yp", [NPT, D], "ExternalOutput")
    ys = dt("ys", [LS, D], "ExternalOutput")
    nshp = dt("nshp", [NL, NH, 128, 128], "ExternalOutput")
    ncp = dt("ncp", [NL, 2, 1024], "ExternalOutput")
    nshs = dt("nshs", [NL, NSEQ, NH, 128, 128], "ExternalOutput")
    ncs = dt("ncs", [NL, NSEQ * 2, 1024], "ExternalOutput")
    sbnd = dt("sbnd", [NL, NH, 128, 128], "Internal")
    wscr = nc.dram_tensor("wscr", [NL, 80, 128, DC * 128], BF16, kind="Internal").ap()

    st = ExitStack()
    sbt = lambda n, s, d: st.enter_context(nc.sbuf_tensor(n, s, d))
    hT = sbt("hT", [128, DC, TM], F32)
    mx = sbt("mx", [128, DC, TM], BF16)
    O_HN = 0
    W_HN = DC * TM // 2
    O_T = [W_HN + i * TM for i in range(3)]
    O_G = W_HN + 3 * TM
    O_B = O_G + TM + 2
    WB = TM // 2
    O_VT = O_B + 5 * WB
    W_VT = (NCHM + 1) * 64
    PMAX = max(c1 - c0 for T_ in (T0, T1) for (c0, c1) in split_pieces(T_))
    PSQ_ALIAS = (O_B + 3 * WB >= W_HN + DC * PMAX)
    AW = max(O_VT + 2 * W_VT, W_HN + DC * PMAX) + 2 * 256 + (0 if PSQ_ALIAS else 2 * 256)
    arena = sbt("arena", [128, AW], F32)
    hn = arena[:, O_HN:O_HN + W_HN].bitcast(BF16).rearrange("p (c t) -> p c t", t=TM)
    sqw = [arena[:, AW - 512 + i * 256: AW - 512 + (i + 1) * 256].bitcast(BF16) for i in range(2)]
    psq = None if PSQ_ALIAS else [arena[:, AW - 1024 + i * 256: AW - 1024 + (i + 1) * 256].bitcast(BF16) for i in range(2)]
    t1, t2, t3 = [arena[:, o:o + TM] for o in O_T]
    if TM >= 1024:
        g1, g2, g3 = t1, t2, t3
    else:
        g1, g2, g3 = [sbt("stg%d" % i, [128, 1024], F32)[:] for i in range(3)]
    Gb = arena[:, O_G:O_G + TM + 2]
    bfb = [arena[:, O_B + i * WB:O_B + (i + 1) * WB].bitcast(BF16) for i in range(5)]
    qt, kt, kh, gate, vT = bfb
    vtok = arena[:, O_VT:O_VT + W_VT].bitcast(BF16).rearrange("p (c v) -> p c v", v=128)
    khtok = arena[:, O_VT + W_VT:O_VT + 2 * W_VT].bitcast(BF16).rearrange("p (c v) -> p c v", v=128)
    wsl = [sbt("wsl%d" % i, [128, DC, 128], BF16) for i in range(NW)]
    smask = sbt("smask", [128, TM], BF16)
    oTb = sbt("oTb", [128, TM], F32)[:]
    onec = sbt("onec", [128, 1], F32)
    Sb = [sbt("S%d" % i, [128, 128], F32) for i in range(2)]
    Sbf = [sbt("Sbf%d" % i, [128, 128], BF16) for i in range(3)]
    Am = [sbt("Am%d" % i, [CH, CH], BF16) for i in range(2)]
    AmS = sbt("AmS", [64, 64], BF16)
    NSIN = 3
    Sin = [sbt("Sin%d" % i, [128, 4, 128], F32) for i in range(NSIN)]
    SinBf = sbt("SinBf", [128, 4, 128], BF16)
    Vblk = [sbt("Vblk%d" % i, [64, 4, 128], BF16) for i in range(2)]
    seqmask = sbt("seqmask", [64, 16], F32)
    gref = sbt("gref", [128, NCOL], F32)
    eref = sbt("eref", [128, NCOL], F32)
    bcol = sbt("bcol", [128, NCOL], F32)
    acol = sbt("acol", [128, NCOL], F32)
    us = sbt("us", [128, NSEQ, 6], F32)
    scT = sbt("scT", [128, 8, NSEQ * 2], F32)
    uprev = sbt("uprev", [128, NL, 8, 2], F32)
    ncst = sbt("ncst", [128, 8, 2], F32)
    ncsS = sbt("ncsS", [128, 8, NSEQ, 2], F32)
    identf = sbt("identf", [128, 128], F32)
    identb = sbt("identb", [128, 128], BF16)
    onesb = sbt("onesb", [128, 128], BF16)
    maskc = sbt("maskc", [CH, CH], F32)
    masks = sbt("masks", [64, 64], F32)
    preT = sbt("preT", [128, NL * 16], F32)
    postT = sbt("postT", [128, NL * 16], F32)
    lbp = sbt("lbp", [128, NL, 8], F32)
    lbT = sbt("lbT", [128, NL, 8], F32)
    omlb = sbt("omlb", [128, NL, 8], F32)
    nomlb = sbt("nomlb", [128, NL, 8], F32)
    lbtmp = sbt("lbtmp", [128, 3, 8], F32)
    nwT = sbt("nwT", [128, NL], F32)
    cwT = sbt("cwT", [128, NL * 24], F32)
    dummy = sbt("dummyt", [128, 2], F32)
    psb = [st.enter_context(nc.psum_tensor("ps%d" % i, [128, 512], F32)) for i in range(8)]
    RING = [0, 1, 2, 3, 7]
    RA, RP, RO0, RO1 = 4, 5, 6, 6
    ring_i = [0]

    def ring():
        b = RING[ring_i[0] % len(RING)]
        ring_i[0] += 1
        return b

    def PB(b):
        return ('ps', b)

    P = Prog(nc)
    P.bar_kw = dict(ap=dummy[:, 0:1], constant=0.0)
    op = P.op

    op('pool', 'memset', dict(ap=identf[:], constant=1.0), w=['identf'])
    op('pool', 'affine_select', dict(out=identf[:], in_=identf[:], pattern=[[-1, 128]], compare_op=ALU.is_equal, fill=0.0, base=0, channel_multiplier=1),
       w=['identf'])
    op('dve', 'tensor_copy', dict(out=identb[:], in_=identf[:]), r=['identf'], w=['identb'])
    op('dve', 'memset', dict(ap=onesb[:], constant=1.0), w=['onesb'])
    for i_ in range(2):
        op('dve', 'memset', dict(ap=Am[i_][:], constant=0.0), w=[('Am', i_)])
    op('dve', 'memset', dict(ap=onec[:], constant=1.0), w=['onec'])
    op('dve', 'memset', dict(ap=dummy[:], constant=0.0), w=['dummy'])
    op('dve', 'memset', dict(ap=psb[RA][:, :], constant=0.0), w=[PB(RA)])
    op('pool', 'memset', dict(ap=maskc[:], constant=1.0), w=['maskc'])
    op('pool', 'affine_select', dict(out=maskc[:], in_=maskc[:], pattern=[[1, CH]], compare_op=ALU.is_ge, fill=0.0, base=0, channel_multiplier=-1),
       w=['maskc'])
    op('pool', 'memset', dict(ap=seqmask[:], constant=1.0), w=['seqmask'])
    op('pool', 'affine_select', dict(out=seqmask[:], in_=seqmask[:], pattern=[[-4, 16]], compare_op=ALU.is_ge, fill=0.0, base=0, channel_multiplier=1), w=['seqmask'])
    op('pool', 'affine_select', dict(out=seqmask[:], in_=seqmask[:], pattern=[[4, 16]], compare_op=ALU.is_ge, fill=0.0, base=3, channel_multiplier=-1), w=['seqmask'])
    op('pool', 'memset', dict(ap=masks[:], constant=1.0), w=['masks'])
    op('pool', 'affine_select', dict(out=masks[:], in_=masks[:], pattern=[[1, 64]], compare_op=ALU.is_ge, fill=0.0, base=0, channel_multiplier=-1),
       w=['masks'])
    m3 = masks[:].rearrange("p (g t) -> p g t", t=4)
    op('pool', 'affine_select', dict(out=m3, in_=m3, pattern=[[-4, 16], [0, 4]], compare_op=ALU.is_ge, fill=0.0, base=0, channel_multiplier=1),
       w=['masks'])
    op('pool', 'affine_select', dict(out=m3, in_=m3, pattern=[[4, 16], [0, 4]], compare_op=ALU.is_ge, fill=0.0, base=3, channel_multiplier=-1),
       w=['masks'])

    def load_T(src, nrows, dst_ap, key):
        stg = g1[0:nrows, 0:128]
        op('sp', 'dma_start', dict(out=stg, in_=src), w=['t1'], dsem='io0')
        b = ring()
        op('pe', 'transpose', dict(out=psb[b][:, 0:nrows], in_=stg, identity=identf[0:nrows, 0:nrows]),
           r=['t1', 'identf'], w=[PB(b)])
        op('dve', 'tensor_copy', dict(out=dst_ap, in_=psb[b][:, 0:nrows]), w=[PB(b), key])

    load_T(prew[:, :], NL * 16, preT[:], 'preT')
    load_T(postw[:, :], NL * 16, postT[:], 'postT')
    load_T(lb_param[:, :], NL * 8, lbp[:].rearrange("p l h -> p (l h)"), 'lbp')
    load_T(hnw[:, :], NL, nwT[:], 'nwT')
    load_T(conv_w[:, :], NL * 24, cwT[:], 'cwT')
    mxl, ssum, rs = lbtmp[:, 0, :], lbtmp[:, 1, :], lbtmp[:, 2, :]
    op('dve', 'tensor_copy', dict(out=mxl, in_=lbp[:, 0, :]), r=['lbp'], w=['lbtmp'])
    for l in range(1, NL):
        op('dve', 'tensor_tensor', dict(out=mxl, in0=mxl, in1=lbp[:, l, :], op=ALU.max),
           r=['lbp'], w=['lbtmp'])
    for l in range(NL):
        op('dve', 'tensor_tensor', dict(out=lbp[:, l, :], in0=lbp[:, l, :], in1=mxl, op=ALU.subtract),
           r=['lbtmp'], w=['lbp'])
    op('act', 'activation', dict(out=lbp[:], in_=lbp[:], func=AF.Exp), w=['lbp'])
    op('dve', 'tensor_copy', dict(out=ssum, in_=lbp[:, 0, :]), r=['lbp'], w=['lbtmp'])
    for l in range(1, NL):
        op('dve', 'tensor_tensor', dict(out=ssum, in0=ssum, in1=lbp[:, l, :], op=ALU.add),
           r=['lbp'], w=['lbtmp'])
    op('dve', 'reciprocal', dict(out=rs, in_=ssum), w=['lbtmp'])
    op('dve', 'memset', dict(ap=lbT[:, 0, :], constant=0.0), w=['lbT'])
    for l in range(1, NL):
        op('dve', 'tensor_tensor', dict(out=lbp[:, l, :], in0=lbp[:, l, :], in1=rs, op=ALU.mult),
           r=['lbtmp'], w=['lbp'])
        op('dve', 'tensor_tensor', dict(out=lbT[:, l, :], in0=lbT[:, l - 1, :], in1=lbp[:, l, :], op=ALU.add),
           r=['lbp'], w=['lbT'])
    op('dve', 'tensor_scalar', dict(out=omlb[:], in0=lbT[:], scalar1=-1.0, scalar2=1.0, op0=ALU.mult, op1=ALU.add),
       r=['lbT'], w=['omlb'])
    op('dve', 'tensor_scalar', dict(out=nomlb[:], in0=lbT[:], scalar1=1.0, scalar2=-1.0, op0=ALU.mult, op1=ALU.add),
       r=['lbT'], w=['nomlb'])
    CONSTS = ['identf', 'identb', 'onesb', 'maskc', 'masks', 'preT', 'postT', 'lbT', 'omlb', 'nomlb', 'nwT', 'cwT']

    wstate = dict(n=0)

    wseen = set()

    def wload(item):
        l_, eid, src3 = item
        k = wstate['n'] % NW
        wstate['n'] += 1
        qt_ = [('w', k, c0) for c0 in range(0, DC, 4)]
        if (l_, eid) not in wseen:
            wseen.add((l_, eid))
            for c0 in range(0, DC, 4):
                op('pool', 'dma_start', dict(out=wsl[k][:, c0:c0 + 4, :], in_=src3[:, c0:c0 + 4, :]),
                   w=[('w', k, c0)], dsem='w%d_%d' % (k, c0))
            op('sp', 'dma_start', dict(out=wscr[l_, eid], in_=wsl[k][:].rearrange("p c e -> p (c e)")),
               r=qt_, w=[('wscr', l_, eid)], dsem='wst%d' % k)
        else:
            op('sp', 'dma_start', dict(out=wsl[k][:].rearrange("p c e -> p (c e)"), in_=wscr[l_, eid]),
               r=[('wscr', l_, eid)], w=qt_, dsem='wld%d' % k)
        return k

    def win_src(l, col0):
        return w_in[l].rearrange("(c p) e -> p c e", p=128)[:, :, col0:col0 + 128]

    def wout_src(l, dd):
        return w_out[l].rearrange("(c p) e -> p c e", p=128)[:, :, dd * 128:(dd + 1) * 128]

    for sb in range(2):
        if sb == 0:
            T = T0
            chunks = [(0, 16)] + [(16 + CH * i, CH) for i in range(NPC)]
            samp = None
            Tp = T0
            iot = [(meta[:, :], None, 16, 0)] + \
                  [(xp[i * 128:(i + 1) * 128, :], yp[i * 128:(i + 1) * 128, :], 128, 16 + 128 * i)
                   for i in range(NPCH // 2)]
        else:
            T = T1
            chunks = [(CH * i, CH) for i in range(NPC)]
            samp = 64 * NPCH
            Tp = 64 * NPCH
            hb = NPCH * 64
            iot = [(xp[hb + i * 128:hb + (i + 1) * 128, :], yp[hb + i * 128:hb + (i + 1) * 128, :], 128, 128 * i)
                   for i in range(NPCH // 2)] + [(xs[:, :], ys[:, :], LS, samp)]
        nch = len(chunks)
        pieces = split_pieces(T)
        pch_lo = 1 if sb == 0 else 0
        pc0 = chunks[pch_lo][0]
        npc = nch - pch_lo

        def v64(ap, lo=pc0, n=npc):
            return ap[:, lo:lo + CH * n].rearrange("p (c l) -> p c l", l=CH)

        P.epoch += 1
        op('dve', 'memset', dict(ap=smask[:], constant=1.0), w=['smask'])
        if sb == 0:
            op('dve', 'memset', dict(ap=smask[:, 0:1], constant=0.0), w=['smask'])
        op('dve', 'memset', dict(ap=v64(smask)[:, :, 0:1], constant=0.0), w=['smask'])
        if samp is not None:
            op('dve', 'memset', dict(ap=smask[:, samp:samp + LS].rearrange("p (j t) -> p j t", t=4)[:, :, 0:1], constant=0.0),
               w=['smask'])
        P.barrier()
        for ti, (src, _dst, ntok, col0) in enumerate(iot):
            for half in range(2):
                stg_full = (g2, g3)[half]
                key = ('t2', 't3')[half]
                stg = stg_full[0:ntok, 0:1024]
                op('sp', 'dma_start', dict(out=stg, in_=src[:, half * 1024:(half + 1) * 1024]),
                   r=['ARENA'], w=[key], dsem='io%d' % (1 + half))
                for q in range(2):
                    b = ring()
                    for j in range(4):
                        dcl = q * 4 + j
                        op('pe', 'transpose', dict(out=psb[b][:, j * 128:j * 128 + ntok], in_=stg[:, dcl * 128:(dcl + 1) * 128], identity=identf[0:ntok, 0:ntok]), r=[key, 'identf'], w=[PB(b)])
                    d0 = half * 8 + q * 4
                    eng = 'act' if q == 0 else 'dve'
                    srcp = psb[b][:, :].rearrange("p (j t) -> p j t", t=128)[:, :, 0:ntok]
                    dstp = hT[:, d0:d0 + 4, col0:col0 + ntok]
                    if eng == 'act':
                        op('act', 'activation', dict(out=dstp, in_=srcp, func=AF.Copy),
                           w=[PB(b)] + [('hT', d0 + j) for j in range(4)])
                    else:
                        op('dve', 'tensor_copy', dict(out=dstp, in_=srcp),
                           w=[PB(b)] + [('hT', d0 + j) for j in range(4)])

        for l in range(NL):
            P.epoch += 1
            def prenorm_gen(lp, pi):
                c0, c1 = pieces[pi]
                N = c1 - c0
                b = RO0
                for d in range(DC):
                    if PSQ_ALIAS:
                        sqb = bfb[3 + d % 2][:, 0:N]
                        sk = ('bfb', 3 + d % 2)
                    else:
                        sqb = psq[d % 2][:, 0:N]
                        sk = ('psq', d % 2)
                    op('act', 'activation', dict(out=sqb, in_=hT[:, d, c0:c1], func=AF.Square),
                       r=[('hT', d)], w=[sk])
                    yield
                    op('pe', 'matmul', dict(out=psb[b][:, 0:N], lhsT=onesb[:], rhs=sqb, start=(d == 0), stop=(d == DC - 1)),
                       r=[sk, 'onesb'], w=[PB(b)])
                op('dve', 'tensor_scalar', dict(out=oTb[:, c0:c1], in0=psb[b][:, 0:N], scalar1=1.0 / D, scalar2=EPS, op0=ALU.mult, op1=ALU.add),
                   w=[PB(b), 'oT'])
                op('act', 'activation', dict(out=oTb[:, c0:c1], in_=oTb[:, c0:c1], func=AF.Ln), w=['oT'])
                op('act', 'activation', dict(out=oTb[:, c0:c1], in_=oTb[:, c0:c1], func=AF.Exp, scale=-0.5), w=['oT'])
                for d in range(DC):
                    op('dve', 'scalar_tensor_tensor', dict(out=hn[:, d, c0:c1], in0=hT[:, d, c0:c1], scalar=preT[:, lp * 16 + d:lp * 16 + d + 1], in1=oTb[:, c0:c1], op0=ALU.mult, op1=ALU.mult),
                       r=[('hT', d), 'oT', 'preT'], w=[('hn', d, pi)])

            if l == 0:
                P.barrier()
                for pi in range(len(pieces)):
                    for _ in prenorm_gen(0, pi):
                        pass

            def proj(slot):
                res = []
                for (c0, c1) in pieces:
                    b = ring()
                    N = c1 - c0
                    for d in range(DC):
                        op('pe', 'matmul', dict(out=psb[b][:, 0:N], lhsT=wsl[slot][:, d, :], rhs=hn[:, d, c0:c1], start=(d == 0), stop=(d == DC - 1)),
                           r=[('w', slot, (d // 4) * 4), ('hn', d)], w=[PB(b)])
                    res.append((b, c0, c1))
                return res

            wq = []
            for h in range(NH):
                for qi in (1, 2, 3, 0):
                    wq.append((l, qi * 8 + h, win_src(l, qi * 1024 + h * 128)))
                for qi in (5, 6, 4, 7):
                    wq.append((l, qi * 8 + h, win_src(l, qi * 1024 + h * 128)))
            for pi_ in range(len(pieces)):
                for dd in range(DC):
                    wq.append((l, 64 + dd, wout_src(l, dd)))
            wpos = dict(i=0, slots=[])

            def wnext():
                while len(wpos['slots']) < NW - 1 + 1 and wpos['i'] < len(wq):
                    wpos['slots'].append(wload(wq[wpos['i']]))
                    wpos['i'] += 1
                return wpos['slots'].pop(0)

            if samp is not None:
                stg = g1[0:NSEQ * 2, 0:1024]
                op('sp', 'dma_start', dict(out=stg, in_=sc[l]), r=['ARENA'], w=['t1'], dsem='io0')
                b = ring()
                for cb in range(8):
                    op('pe', 'transpose', dict(out=psb[b][:, cb * NSEQ * 2:(cb + 1) * NSEQ * 2], in_=stg[:, cb * 128:(cb + 1) * 128], identity=identf[0:NSEQ * 2, 0:NSEQ * 2]), r=['t1', 'identf'], w=[PB(b)])
                op('dve', 'tensor_copy', dict(out=scT[:].rearrange("p c n -> p (c n)"), in_=psb[b][:, 0:8 * NSEQ * 2]),
                   w=[PB(b), 'scT'])

            def proj_gen(slot):
                for pi_, (c0, c1) in enumerate(pieces):
                    b = ring()
                    N = c1 - c0
                    for d in range(DC):
                        op('pe', 'matmul', dict(out=psb[b][:, 0:N], lhsT=wsl[slot][:, d, :], rhs=hn[:, d, c0:c1], start=(d == 0), stop=(d == DC - 1)),
                           r=[('w', slot, (d // 4) * 4), ('hn', d, pi_)], w=[PB(b)])
                    yield (b, c0, c1)

            tchunks = list(chunks) + ([(samp, LS)] if samp is not None else [])

            def tok_major(srcb, srck, dst, dstk, eng):
                for g0 in range(0, len(tchunks), 8):
                    grp = tchunks[g0:g0 + 8]
                    b = ring()
                    pbv = psb[b][:, :].bitcast(BF16)
                    for j, (cc0, L) in enumerate(grp):
                        op('pe', 'transpose', dict(out=pbv[0:L, j * 128:(j + 1) * 128], in_=srcb[:, cc0:cc0 + L], identity=identb[:]),
                           r=[srck, 'identb'], w=[PB(b)])
                    j = 0
                    while j < len(grp):
                        L = grp[j][1]
                        j1 = j
                        while j1 < len(grp) and grp[j1][1] == L:
                            j1 += 1
                        srcp = pbv[0:L, j * 128:j1 * 128].rearrange("p (c v) -> p c v", v=128)
                        dstp = dst[0:L, g0 + j:g0 + j1, :]
                        if eng == 'act':
                            op('act', 'activation', dict(out=dstp, in_=srcp, func=AF.Copy), w=[PB(b), dstk])
                        else:
                            op('dve', 'tensor_copy', dict(out=dstp, in_=srcp), w=[PB(b), dstk])
                        j = j1

            def front(h, side=None):
                lbc = lbT[:, l, h:h + 1]
                omc = omlb[:, l, h:h + 1]
                nomc = nomlb[:, l, h:h + 1]
                sl = lambda ap: ap[:, 0:T]
                slot = wnext()
                for (b, c0, c1) in proj_gen(slot):
                    op('act', 'activation', dict(out=t1[:, c0:c1], in_=psb[b][:, 0:c1 - c0], func=AF.Exp, scale=-1.0),
                       w=[PB(b), 't1'])
                    if side is not None:
                        next(side, None)
                if side is not None:
                    for _ in side:
                        pass
                op('dve', 'tensor_scalar', dict(out=sl(t1), in0=sl(t1), scalar1=1e30, scalar2=1.0, op0=ALU.min, op1=ALU.add), w=['t1'])
                op('act', 'activation', dict(out=sl(t1), in_=sl(t1), func=AF.Ln), w=['t1'])
                op('act', 'activation', dict(out=sl(t1), in_=sl(t1), func=AF.Exp, scale=-1.0), w=['t1'])
                slot = wnext()
                for (b, c0, c1) in proj_gen(slot):
                    op('act', 'activation', dict(out=vT[:, c0:c1], in_=psb[b][:, 0:c1 - c0], func=AF.Copy),
                       w=[PB(b), ('bfb', 4)])
                op('dve', 'tensor_scalar', dict(out=sl(t2), in0=sl(t1), scalar1=omc, scalar2=lbc, op0=ALU.mult, op1=ALU.add),
                   r=['t1', 'omlb', 'lbT'], w=['t2'])
                op('act', 'activation', dict(out=sl(t2), in_=sl(t2), func=AF.Ln), w=['t2'])
                op('dve', 'tensor_scalar', dict(out=sl(t1), in0=sl(t1), scalar1=nomc, scalar2=omc, op0=ALU.mult, op1=ALU.add),
                   r=['nomlb', 'omlb'], w=['t1'])
                Gv = Gb[:, 0:T]
                op('dve', 'tensor_tensor_scan', dict(out=Gv, data0=smask[:, 0:T], data1=sl(t2), initial=0.0, op0=ALU.mult, op1=ALU.add),
                   r=['t2', 'smask'], w=['G'])
                if sb == 0:
                    op('dve', 'tensor_copy', dict(out=gref[:, 0:1], in_=Gb[:, 7:8]), r=['G'], w=['gref'])
                op('dve', 'tensor_copy', dict(out=gref[:, pch_lo:nch], in_=v64(Gb)[:, :, CH // 2 - 1]), r=['G'], w=['gref'])
                if sb == 0:
                    op('dve', 'tensor_scalar', dict(out=Gb[:, 0:16], in0=Gb[:, 0:16], scalar1=gref[:, 0:1], scalar2=None, op0=ALU.subtract),
                       r=['gref'], w=['G'])
                op('dve', 'tensor_tensor', dict(out=v64(Gb), in0=v64(Gb), in1=gref[:, pch_lo:nch].unsqueeze(2).to_broadcast([128, npc, CH]), op=ALU.subtract),
                   r=['gref'], w=['G'])
                if sb == 0:
                    op('dve', 'tensor_tensor', dict(out=bcol[:, 0:1], in0=Gb[:, 15:16], in1=gref[:, 0:1], op=ALU.add), r=['G', 'gref'], w=['bcol'])
                op('dve', 'tensor_tensor', dict(out=bcol[:, pch_lo:nch], in0=v64(Gb)[:, :, CH - 1], in1=gref[:, pch_lo:nch], op=ALU.add),
                   r=['G', 'gref'], w=['bcol'])
                ncol_used = nch
                if samp is not None:
                    op('dve', 'tensor_copy', dict(out=bcol[:, nch:nch + NSEQ], in_=Gb[:, samp:samp + LS].rearrange("p (j t) -> p j t", t=4)[:, :, 3]),
                       r=['G'], w=['bcol'])
                    ncol_used = nch + NSEQ
                op('act', 'activation', dict(out=bcol[:, 0:ncol_used], in_=bcol[:, 0:ncol_used], func=AF.Exp), w=['bcol'])
                op('act', 'activation', dict(out=eref[:, 0:nch], in_=gref[:, 0:nch], func=AF.Exp), r=['gref'], w=['eref'])
                op('act', 'activation', dict(out=sl(t2), in_=Gv, func=AF.Exp), r=['G'], w=['t2'])
                op('act', 'activation', dict(out=sl(t3), in_=Gv, func=AF.Exp, scale=-1.0), r=['G'], w=['t3'])
                if sb == 0:
                    op('dve', 'tensor_copy', dict(out=acol[:, 0:1], in_=t2[:, 15:16]), r=['t2'], w=['acol'])
                op('dve', 'tensor_copy', dict(out=acol[:, pch_lo:nch], in_=v64(t2)[:, :, CH - 1]), r=['t2'], w=['acol'])
                if samp is not None:
                    op('dve', 'tensor_copy', dict(out=acol[:, nch:nch + NSEQ], in_=t2[:, samp:samp + LS].rearrange("p (j t) -> p j t", t=4)[:, :, 3]),
                       r=['t2'], w=['acol'])
                slot = wnext()
                for (b, c0, c1) in proj_gen(slot):
                    N = c1 - c0
                    op('act', 'activation', dict(out=oTb[:, c0:c1], in_=psb[b][:, 0:N], func=AF.Exp, scale=-1.0),
                       w=[PB(b), 'oT'])
                    op('act', 'activation', dict(out=oTb[:, c0:c1], in_=oTb[:, c0:c1], func=AF.Ln, bias=onec[:, 0:1]), r=['onec'], w=['oT'])
                    op('act', 'activation', dict(out=oTb[:, c0:c1], in_=oTb[:, c0:c1], func=AF.Exp, scale=-1.0), w=['oT'])
                    op('dve', 'tensor_tensor', dict(out=gate[:, c0:c1], in0=psb[b][:, 0:N], in1=oTb[:, c0:c1], op=ALU.mult),
                       r=['oT'], w=[PB(b), ('bfb', 3)])
                tok_major(vT, ('bfb', 4), vtok, 'vtok', 'act')
                slot = wnext()
                for (b, c0, c1) in proj_gen(slot):
                    op('dve', 'tensor_tensor', dict(out=qt[:, c0:c1], in0=psb[b][:, 0:c1 - c0], in1=t2[:, c0:c1], op=ALU.mult),
                       r=['t2'], w=[PB(b), ('bfb', 0)])
                op('dve', 'tensor_tensor', dict(out=sl(kt), in0=sl(t1), in1=sl(t3), op=ALU.mult), r=['t1', 't3'], w=[('bfb', 1)])
                if sb == 0:
                    op('dve', 'tensor_scalar', dict(out=kh[:, 0:16], in0=kt[:, 0:16], scalar1=acol[:, 0:1], scalar2=None, op0=ALU.mult),
                       r=[('bfb', 1), 'acol'], w=[('bfb', 2)])
                op('dve', 'tensor_tensor', dict(out=v64(kh), in0=v64(kt), in1=acol[:, pch_lo:nch].unsqueeze(2).to_broadcast([128, npc, CH]), op=ALU.mult),
                   r=[('bfb', 1), 'acol'], w=[('bfb', 2)])
                if samp is not None:
                    s4 = lambda ap: ap[:, samp:samp + LS].rearrange("p (j t) -> p j t", t=4)
                    op('dve', 'tensor_tensor', dict(out=s4(kh), in0=s4(kt), in1=acol[:, nch:nch + NSEQ].unsqueeze(2).to_broadcast([128, NSEQ, 4]), op=ALU.mult),
                       r=[('bfb', 1), 'acol'], w=[('bfb', 2)])

            def sin_load(h, qd):
                j0 = qd * 4
                si = (h * (NSEQ // 4) + qd) % NSIN
                op('sp', 'dma_start', dict(out=Sin[si][:], in_=sh[l, j0:j0 + 4, h].rearrange("j k v -> k j v")),
                   w=[('Sin', si)], dsem='Sin%d' % si)

            def vblk_build(qd):
                j0 = qd * 4
                op('pool', 'tensor_tensor', dict(out=Vblk[qd % 2][0:LS], in0=vtok[0:LS, nch, :].unsqueeze(1).to_broadcast([LS, 4, 128]),
                                                in1=seqmask[0:LS, j0:j0 + 4].unsqueeze(2).to_broadcast([LS, 4, 128]), op=ALU.mult),
                   r=['vtok', 'seqmask'], w=[('Vblk', qd % 2)])

            def chain_gen(h):
                oT = oTb
                if samp is not None:
                    sin_load(h, 0)
                    sin_load(h, 1)
                for _ in range(CHAIN_DELAY):
                    yield
                tok_major(kh, ('bfb', 2), khtok, 'khtok', 'dve')
                if samp is not None:
                    vblk_build(0)
                    if NSEQ // 4 > 1:
                        vblk_build(1)
                if sb == 0:
                    op('dve', 'memset', dict(ap=Sb[0][:], constant=0.0), w=[('S', 0)])
                else:
                    op('sp', 'dma_start', dict(out=Sb[0][:], in_=sbnd[l, h]), r=['sbnd%d_%d' % (l, h)], w=[('S', 0)], dsem='S0')
                op('act', 'activation', dict(out=Sbf[0][:], in_=Sb[0][:], func=AF.Identity, scale=eref[:, 0:1]),
                   r=[('S', 0), 'eref'], w=[('Sbf', 0)])
                ob_i = 0
                win = None
                for ci in range(nch + 1):
                    if ci < nch:
                        cc0, L = chunks[ci]
                        a_i = ci % 2
                        n_i = (ci + 1) % 2
                        if L > 64:
                            Hh = L // 2
                            op('pe', 'matmul', dict(out=psb[RA][0:L, Hh:L], lhsT=kt[:, cc0:cc0 + L], rhs=qt[:, cc0 + Hh:cc0 + L], start=True, stop=True),
                               r=[('bfb', 1), ('bfb', 0)], w=[PB(RA)])
                            op('pe', 'matmul', dict(out=psb[RA][0:Hh, 0:Hh], lhsT=kt[:, cc0:cc0 + Hh], rhs=qt[:, cc0:cc0 + Hh], start=True, stop=True),
                               r=[('bfb', 1), ('bfb', 0)], w=[PB(RA)])
                        else:
                            op('pe', 'matmul', dict(out=psb[RA][0:L, 0:L], lhsT=kt[:, cc0:cc0 + L], rhs=qt[:, cc0:cc0 + L], start=True, stop=True),
                               r=[('bfb', 1), ('bfb', 0)], w=[PB(RA)])
                        op('pe', 'matmul', dict(out=psb[RP][:, 0:128], lhsT=khtok[0:L, ci, :], rhs=vtok[0:L, ci, :], start=True, stop=True),
                           r=['khtok', 'vtok'], w=[PB(RP)])
                        op('dve', 'copy_predicated', dict(out=Am[a_i][0:L, 0:L], mask=maskc[0:L, 0:L].bitcast(mybir.dt.uint32), data=psb[RA][0:L, 0:L]),
                           r=['maskc'], w=[PB(RA), ('Am', a_i)])
                        op('dve', 'scalar_tensor_tensor', dict(out=Sb[n_i][:], in0=Sb[ci % 2][:], scalar=bcol[:, ci:ci + 1], in1=psb[RP][:, 0:128], op0=ALU.mult, op1=ALU.add),
                           r=[('S', ci % 2), 'bcol'], w=[PB(RP), ('S', n_i)])
                        if ci + 1 < nch:
                            op('act', 'activation', dict(out=Sbf[(ci + 1) % 3][:], in_=Sb[n_i][:], func=AF.Identity, scale=eref[:, ci + 1:ci + 2]),
                               r=[('S', n_i), 'eref'], w=[('Sbf', (ci + 1) % 3)])
                    if ci >= 1:
                        cj = ci - 1
                        cc0, L = chunks[cj]
                        if win is None:
                            win = [(RO0, RO1)[ob_i % 2], cc0]
                            ob_i += 1
                        ob, w0 = win
                        oc = cc0 - w0
                        op('pe', 'matmul', dict(out=psb[ob][:, oc:oc + L], lhsT=vtok[0:L, cj, :], rhs=Am[cj % 2][0:L, 0:L], start=True, stop=False),
                           r=['vtok', ('Am', cj % 2)], w=[PB(ob)])
                        op('pe', 'matmul', dict(out=psb[ob][:, oc:oc + L], lhsT=Sbf[cj % 3][:], rhs=qt[:, cc0:cc0 + L], start=False, stop=True),
                           r=[('Sbf', cj % 3), ('bfb', 0)], w=[PB(ob)])
                        nxt_end = (chunks[cj + 1][0] + chunks[cj + 1][1] - w0) if cj + 1 < nch else None
                        if nxt_end is None or nxt_end > 512:
                            wlen = cc0 + L - w0
                            op('act', 'activation', dict(out=oT[:, w0:w0 + wlen], in_=psb[ob][:, 0:wlen], func=AF.Copy),
                               w=[PB(ob), 'oT'])
                            win = None
                    yield
                fin = nch % 2
                if sb == 0:
                    op('sp', 'dma_start', dict(out=sbnd[l, h], in_=Sb[fin][:]), r=[('S', fin)], w=['sbnd%d_%d' % (l, h)], dsem='Sst')
                else:
                    op('sp', 'dma_start', dict(out=nshp[l, h], in_=Sb[fin][:]), r=[('S', fin)], dsem='Sst')
                if samp is not None:
                    ci_s = nch
                    op('pe', 'matmul', dict(out=psb[RA][0:LS, 0:LS], lhsT=kt[:, samp:samp + LS], rhs=qt[:, samp:samp + LS], start=True, stop=True),
                       r=[('bfb', 1), ('bfb', 0)], w=[PB(RA)])
                    op('dve', 'tensor_tensor', dict(out=AmS[0:LS, 0:LS], in0=psb[RA][0:LS, 0:LS], in1=masks[0:LS, 0:LS], op=ALU.mult),
                       r=['masks'], w=[PB(RA), 'AmS'])
                    ob = (RO0, RO1)[ob_i % 2]
                    ob_i += 1
                    op('pe', 'matmul', dict(out=psb[ob][:, 0:LS], lhsT=vtok[0:LS, ci_s, :], rhs=AmS[0:LS, 0:LS], start=True, stop=False),
                       r=['vtok', 'AmS'], w=[PB(ob)])
                    yield
                    NQ = NSEQ // 4
                    for qd in range(NQ):
                        j0 = qd * 4
                        si = (h * NQ + qd) % NSIN
                        sk = ('Sin', si)
                        Sq = Sin[si]
                        if qd + 2 < NQ:
                            sin_load(h, qd + 2)
                        op('act', 'activation', dict(out=SinBf[:], in_=Sq[:], func=AF.Copy), r=[sk], w=['SinBf'])
                        Vb = Vblk[qd % 2]
                        vk = ('Vblk', qd % 2)
                        yield
                        for jq in range(4):
                            j = j0 + jq
                            last = (j == NSEQ - 1)
                            op('pe', 'matmul', dict(out=psb[ob][:, 4 * j:4 * j + 4], lhsT=SinBf[:, jq, :], rhs=qt[:, samp + 4 * j:samp + 4 * j + 4], start=False, stop=last),
                               r=['SinBf', ('bfb', 0)], w=[PB(ob)])
                        op('pe', 'matmul', dict(out=psb[RP][:, 0:512], lhsT=khtok[0:LS, ci_s, :], rhs=Vb[0:LS].rearrange("p j v -> p (j v)"), start=True, stop=True),
                           r=['khtok', vk], w=[PB(RP)])
                        if qd + 2 < NQ:
                            vblk_build(qd + 2)
                        for jq in range(4):
                            j = j0 + jq
                            op('dve', 'scalar_tensor_tensor', dict(out=Sq[:, jq, :], in0=Sq[:, jq, :], scalar=bcol[:, nch + j:nch + j + 1], in1=psb[RP][:, jq * 128:(jq + 1) * 128], op0=ALU.mult, op1=ALU.add),
                               r=['bcol'], w=[PB(RP), sk])
                        op('sp', 'dma_start', dict(out=nshs[l, j0:j0 + 4, h].rearrange("j k v -> k j v"), in_=Sq[:]),
                           r=[sk], dsem='So%d' % si)
                        yield
                    op('act', 'activation', dict(out=oT[:, samp:samp + LS], in_=psb[ob][:, 0:LS], func=AF.Copy),
                       w=[PB(ob), 'oT'])
                    yield

            def norm_gen(h):
                oT = oTb
                for (c0, c1) in pieces:
                    op('act', 'activation', dict(out=kt[:, c0:c1], in_=oT[:, c0:c1], func=AF.Square),
                       r=['oT'], w=[('bfb', 1)])
                yield
                banks = []
                for (c0, c1) in pieces:
                    N = c1 - c0
                    b = ring()
                    banks.append(b)
                    op('pe', 'matmul', dict(out=psb[b][:, 0:N], lhsT=onesb[:], rhs=kt[:, c0:c1], start=True, stop=True),
                       r=[('bfb', 1), 'onesb'], w=[PB(b)])
                yield
                for b, (c0, c1) in zip(banks, pieces):
                    N = c1 - c0
                    op('dve', 'tensor_scalar', dict(out=t3[:, c0:c1], in0=psb[b][:, 0:N], scalar1=1.0 / 128, scalar2=EPS, op0=ALU.mult, op1=ALU.add),
                       w=[PB(b), 't3'])
                    op('act', 'activation', dict(out=t3[:, c0:c1], in_=t3[:, c0:c1], func=AF.Ln), w=['t3'])
                    op('act', 'activation', dict(out=t3[:, c0:c1], in_=t3[:, c0:c1], func=AF.Exp, scale=-0.5), w=['t3'])
                    op('dve', 'scalar_tensor_tensor', dict(out=oT[:, c0:c1], in0=oT[:, c0:c1], scalar=nwT[:, l:l + 1], in1=t3[:, c0:c1], op0=ALU.mult, op1=ALU.mult),
                       r=['t3', 'nwT'], w=['oT'])
                    op('dve', 'tensor_tensor', dict(out=mx[:, h, c0:c1], in0=oT[:, c0:c1], in1=gate[:, c0:c1], op=ALU.mult),
                       r=['oT', ('bfb', 3)], w=[('mx', h)])

            def cb_gen(cb):
                w0c = cwT[:, l * 24 + 0 * 8 + cb:l * 24 + 0 * 8 + cb + 1]
                w1c = cwT[:, l * 24 + 1 * 8 + cb:l * 24 + 1 * 8 + cb + 1]
                w2c = cwT[:, l * 24 + 2 * 8 + cb:l * 24 + 2 * 8 + cb + 1]
                ub = Gb
                slot = wnext()
                for (b, c0, c1) in proj_gen(slot):
                    op('act', 'activation', dict(out=t1[:, c0:c1], in_=psb[b][:, 0:c1 - c0], func=AF.Copy),
                       w=[PB(b), 't1'])
                    yield
                slot = wnext()
                for (b, c0, c1) in proj_gen(slot):
                    op('dve', 'tensor_tensor', dict(out=ub[:, 2 + c0:2 + c1], in0=psb[b][:, 0:c1 - c0], in1=t1[:, c0:c1], op=ALU.mult),
                       r=['t1'], w=[PB(b), 'G'])
                    yield
                if sb == 0:
                    op('dve', 'memset', dict(ap=ub[:, 0:2], constant=0.0), w=['G'])
                else:
                    op('dve', 'tensor_copy', dict(out=ub[:, 0:2], in_=uprev[:, l, cb, :]), r=['uprev'], w=['G'])
                op('dve', 'tensor_scalar', dict(out=t2[:, 0:Tp], in0=ub[:, 2:2 + Tp], scalar1=w2c, scalar2=None, op0=ALU.mult),
                   r=['G', 'cwT'], w=['t2'])
                op('dve', 'scalar_tensor_tensor', dict(out=t2[:, 0:Tp], in0=ub[:, 1:1 + Tp], scalar=w1c, in1=t2[:, 0:Tp], op0=ALU.mult, op1=ALU.add),
                   r=['G', 'cwT'], w=['t2'])
                op('dve', 'scalar_tensor_tensor', dict(out=t2[:, 0:Tp], in0=ub[:, 0:Tp], scalar=w0c, in1=t2[:, 0:Tp], op0=ALU.mult, op1=ALU.add),
                   r=['G', 'cwT'], w=['t2'])
                if sb == 0:
                    op('dve', 'tensor_copy', dict(out=uprev[:, l, cb, :], in_=ub[:, Tp:Tp + 2]), r=['G'], w=['uprev'])
                else:
                    op('dve', 'tensor_copy', dict(out=ncst[:, cb, :], in_=ub[:, Tp:Tp + 2]), r=['G'], w=['ncst'])
                if samp is not None:
                    op('dve', 'tensor_copy', dict(out=us[:, :, 0:2], in_=scT[:, cb, :].rearrange("p (j r) -> p j r", r=2)),
                       r=['scT'], w=['us'])
                    op('dve', 'tensor_copy', dict(out=us[:, :, 2:6], in_=ub[:, 2 + samp:2 + samp + LS].rearrange("p (j t) -> p j t", t=4)),
                       r=['G'], w=['us'])
                    t2s = t2[:, samp:samp + LS].rearrange("p (j t) -> p j t", t=4)
                    op('dve', 'tensor_scalar', dict(out=t2s, in0=us[:, :, 2:6], scalar1=w2c, scalar2=None, op0=ALU.mult),
                       r=['us', 'cwT'], w=['t2'])
                    op('dve', 'scalar_tensor_tensor', dict(out=t2s, in0=us[:, :, 1:5], scalar=w1c, in1=t2s, op0=ALU.mult, op1=ALU.add),
                       r=['us', 'cwT'], w=['t2'])
                    op('dve', 'scalar_tensor_tensor', dict(out=t2s, in0=us[:, :, 0:4], scalar=w0c, in1=t2s, op0=ALU.mult, op1=ALU.add),
                       r=['us', 'cwT'], w=['t2'])
                    op('dve', 'tensor_copy', dict(out=ncsS[:, cb, :, :], in_=us[:, :, 4:6]), r=['us'], w=['ncsS'])
                slot = wnext()
                for (b, c0, c1) in proj_gen(slot):
                    op('dve', 'tensor_tensor', dict(out=t2[:, c0:c1], in0=psb[b][:, 0:c1 - c0], in1=t2[:, c0:c1], op=ALU.mult),
                       w=[PB(b), 't2'])
                    yield
                slot = wnext()
                for (b, c0, c1) in proj_gen(slot):
                    N = c1 - c0
                    op('act', 'activation', dict(out=t3[:, c0:c1], in_=psb[b][:, 0:N], func=AF.Exp, scale=-1.0),
                       w=[PB(b), 't3'])
                    op('act', 'activation', dict(out=t3[:, c0:c1], in_=t3[:, c0:c1], func=AF.Ln, bias=onec[:, 0:1]), r=['onec'], w=['t3'])
                    op('act', 'activation', dict(out=t3[:, c0:c1], in_=t3[:, c0:c1], func=AF.Exp, scale=-1.0), w=['t3'])
                    op('dve', 'tensor_tensor', dict(out=t3[:, c0:c1], in0=psb[b][:, 0:N], in1=t3[:, c0:c1], op=ALU.mult),
                       w=[PB(b), 't3'])
                    op('dve', 'tensor_tensor', dict(out=mx[:, 8 + cb, c0:c1], in0=t2[:, c0:c1], in1=t3[:, c0:c1], op=ALU.mult),
                       r=['t2', 't3'], w=[('mx', 8 + cb)])
                    yield

            def merge(ga, gb, ra):
                da = db = False
                while not (da and db):
                    if not db:
                        try:
                            next(gb)
                        except StopIteration:
                            db = True
                    for _ in range(ra):
                        if not da:
                            try:
                                next(ga)
                            except StopIteration:
                                da = True

            side = None
            for i in range(NH):
                front(i, side)
                merge(chain_gen(i), cb_gen(i), CHAIN_RATIO)
                side = norm_gen(i)
            for _ in side:
                pass
            if sb == 1:
                stg = g1[0:2, 0:1024]
                for g in range(2):
                    b = ring()
                    for j in range(4):
                        cb = g * 4 + j
                        op('pe', 'transpose', dict(out=psb[b][0:2, j * 128:(j + 1) * 128], in_=ncst[:, cb, :], identity=identf[:]),
                           r=['ncst', 'identf'], w=[PB(b)])
                    op('dve', 'tensor_copy', dict(out=stg[:, g * 512:(g + 1) * 512], in_=psb[b][0:2, 0:512]),
                       r=['ARENA'], w=[PB(b), 't1'])
                op('sp', 'dma_start', dict(out=ncp[l], in_=stg), r=['t1', 'ARENA'], dsem='io0')
                stg2 = g3[0:NSEQ * 2, 0:1024]
                for g in range(2):
                    b = ring()
                    for j in range(4):
                        cb = g * 4 + j
                        op('pe', 'transpose', dict(out=psb[b][0:NSEQ * 2, j * 128:(j + 1) * 128], in_=ncsS[:, cb, :, :].rearrange("p j r -> p (j r)"), identity=identf[:]),
                           r=['ncsS', 'identf'], w=[PB(b)])
                    op('dve', 'tensor_copy', dict(out=stg2[:, g * 512:(g + 1) * 512], in_=psb[b][0:NSEQ * 2, 0:512]),
                       r=['ARENA'], w=[PB(b), 't3'])
                op('sp', 'dma_start', dict(out=ncs[l], in_=stg2), r=['t3', 'ARENA'], dsem='io3')

            mo_p = arena[:, W_HN:W_HN + DC * PMAX].rearrange("p (c t) -> p c t", t=PMAX)
            MO_TOK = ['t1', 't2', 't3', 'G'] + [('bfb', i_) for i_ in range(5)]
            pend_ones = []

            def flush_ones(keep):
                while len(pend_ones) > keep:
                    kw = pend_ones.pop(0)
                    r_ = kw.pop('_r')
                    w_ = kw.pop('_w')
                    op('pe', 'matmul', kw, r=r_, w=w_)

            side = None
            for pi, (c0, c1) in enumerate(pieces):
                N = c1 - c0
                SQ = (RA, RP)[pi % 2]
                for dd in range(DC):
                    slot = wnext()
                    b = ring()
                    for ec in range(DC):
                        if ec == DC // 2:
                            flush_ones(0)
                        op('pe', 'matmul', dict(out=psb[b][:, 0:N], lhsT=wsl[slot][:, ec, :], rhs=mx[:, ec, c0:c1], start=(ec == 0), stop=(ec == DC - 1)),
                           r=[('w', slot, (ec // 4) * 4), ('mx', ec)], w=[PB(b)])
                    if side is not None:
                        next(side, None)
                    sq = sqw[dd % 2][:, 0:N]
                    sqk = ('sqw', dd % 2)
                    op('act', 'activation', dict(out=sq, in_=psb[b][:, 0:N], func=AF.Square),
                       w=[PB(b), sqk])
                    op('dve', 'tensor_copy', dict(out=mo_p[:, dd, 0:N], in_=psb[b][:, 0:N]),
                       w=[PB(b), ('mo', dd)] + (MO_TOK if (pi == 0 and dd == 0) else []))
                    pend_ones.append(dict(out=psb[SQ][:, 0:N], lhsT=onesb[:], rhs=sq, start=(dd == 0), stop=(dd == DC - 1), _r=[sqk, 'onesb'], _w=[PB(SQ)]))
                flush_ones(0)
                if side is not None:
                    for _ in side:
                        pass
                    side = None
                op('dve', 'tensor_scalar', dict(out=oTb[:, c0:c1], in0=psb[SQ][:, 0:N], scalar1=1.0 / D, scalar2=EPS, op0=ALU.mult, op1=ALU.add),
                   w=[PB(SQ), 'oT'])
                op('act', 'activation', dict(out=oTb[:, c0:c1], in_=oTb[:, c0:c1], func=AF.Ln), w=['oT'])
                op('act', 'activation', dict(out=oTb[:, c0:c1], in_=oTb[:, c0:c1], func=AF.Exp, scale=-0.5), w=['oT'])
                for dd in range(DC):
                    op('dve', 'scalar_tensor_tensor', dict(out=mo_p[:, dd, 0:N], in0=mo_p[:, dd, 0:N], scalar=postT[:, l * 16 + dd:l * 16 + dd + 1], in1=oTb[:, c0:c1], op0=ALU.mult, op1=ALU.mult),
                       r=['oT', 'postT'], w=[('mo', dd)])
                    op('pool', 'tensor_tensor', dict(out=hT[:, dd, c0:c1], in0=hT[:, dd, c0:c1], in1=mo_p[:, dd, 0:N], op=ALU.add),
                       r=[('mo', dd)], w=[('hT', dd)])
                if l + 1 < NL:
                    side = prenorm_gen(l + 1, pi)
            if side is not None:
                for _ in side:
                    pass
            for dd in range(DC):
                P.tok_r.setdefault('t1', [])
            for t_ in MO_TOK:
                lst = P.tok_r.setdefault(t_, [])
                for dd in range(DC):
                    if ('mo', dd) in P.tok_w:
                        lst.append(P.tok_w[('mo', dd)])
                    lst.extend(P.tok_r.get(('mo', dd), []))

        P.barrier()
        sti = 0
        for (src, dst, ntok, col0) in iot:
            if dst is None:
                continue
            for q in range(4):
                b = ring()
                for j in range(4):
                    d = q * 4 + j
                    op('pe', 'transpose', dict(out=psb[b][0:ntok, j * 128:(j + 1) * 128], in_=hT[:, d, col0:col0 + ntok], identity=identf[:]),
                       r=[('hT', d), 'identf'], w=[PB(b)])
                stg_full = (g2, g3)[sti % 2]
                key = ('t2', 't3')[sti % 2]
                stg = stg_full[0:ntok, 0:512]
                if sti % 2 == 0:
                    op('act', 'activation', dict(out=stg, in_=psb[b][0:ntok, 0:512], func=AF.Copy),
                       r=['ARENA'], w=[PB(b), key])
                else:
                    op('dve', 'tensor_copy', dict(out=stg, in_=psb[b][0:ntok, 0:512]),
                       r=['ARENA'], w=[PB(b), key])
                op('sp', 'dma_start', dict(out=dst[:, q * 512:(q + 1) * 512], in_=stg),
                   r=[key, 'ARENA'], dsem='io%d' % (1 + sti % 2))
                sti += 1

    P.emit()
    st.close()
    return nc


_NC_CACHE = {}


def kernel(x_prompt, x_sample, state_hgrn, state_conv, meta_tokens, w_in, conv_w, lb_param,
           hgrn_norm_w, w_out, pre_norm_w, post_norm_w):
    NL = w_in.shape[0]
    B, SEQ, _ = x_prompt.shape
    DECB = x_sample.shape[0]
    NSEQ = DECB // N_CORES
    NPCH = SEQ // 128
    key = (NL, NPCH, NSEQ)
    if key not in _NC_CACHE:
        _NC_CACHE[key] = build_nc(NL=NL, NPCH=NPCH, NSEQ=NSEQ)
    nc = _NC_CACHE[key]
    f = lambda a: np.ascontiguousarray(np.asarray(a, dtype=np.float32))
    common = dict(
        meta=f(meta_tokens), w_in=f(w_in), w_out=f(w_out),
        conv_w=f(conv_w).reshape(NL * 24, 128), lb_param=f(lb_param).reshape(NL * 8, 128),
        hnw=f(hgrn_norm_w), prew=f(pre_norm_w).reshape(NL * 16, 128), postw=f(post_norm_w).reshape(NL * 16, 128))
    xs_all = f(x_sample)
    sh_all = f(state_hgrn)
    sc_all = f(state_conv)
    xp_all = f(x_prompt)
    in_maps = []
    for c in range(N_CORES):
        j0 = c * NSEQ
        m = dict(common)
        m['xp'] = xp_all[c % B]
        m['xs'] = np.ascontiguousarray(xs_all[j0:j0 + NSEQ].reshape(NSEQ * 4, D))
        m['sh'] = np.ascontiguousarray(sh_all[:, j0:j0 + NSEQ])
        m['sc'] = np.ascontiguousarray(sc_all[:, j0:j0 + NSEQ].reshape(NL, NSEQ * 2, 1024))
        in_maps.append(m)
    res = run_bass_kernel_spmd(nc, in_maps, core_ids=list(range(N_CORES)))
    R = res.results
    y_prompt = np.stack([R[b]['yp'] for b in range(B)]).astype(np.float32)
    y_sample = np.concatenate([R[c]['ys'].reshape(NSEQ, 4, D) for c in range(N_CORES)], axis=0).astype(np.float32)
    nhp = np.stack([R[b]['nshp'] for b in range(B)], axis=1).astype(np.float32)
    ncp_ = np.stack([R[b]['ncp'] for b in range(B)], axis=1).astype(np.float32)
    nhs = np.concatenate([R[c]['nshs'] for c in range(N_CORES)], axis=1).astype(np.float32)
    ncs_ = np.concatenate([R[c]['ncs'].reshape(NL, NSEQ, 2, 1024) for c in range(N_CORES)], axis=1).astype(np.float32)
    return (y_prompt, y_sample, nhp, ncp_, nhs, ncs_)
```

```python
from contextlib import ExitStack

import numpy as np
import concourse.bass as bass
import concourse.mybir as mybir
from concourse.bass_utils import run_bass_kernel_spmd

F32 = mybir.dt.float32
BF16 = mybir.dt.bfloat16
AF = mybir.ActivationFunctionType
ALU = mybir.AluOpType

D = 2048
DC = 16
NH = 8
EPS = 1e-6
N_CORES = 8


class Prog:
    ENG = ['pe', 'act', 'dve', 'pool', 'sp']

    def __init__(s, nc):
        s.nc = nc
        s.ops = []
        s.tok_w = {}
        s.tok_r = {}
        s.epoch = 0
        s.dma_cnt = {}
        s.bar = None
        s.bar_kw = None

    def barrier(s):
        last = {}
        for j, o in enumerate(s.ops):
            if o['dsem'] is not None:
                last[('dma', o['dsem'])] = j
            else:
                last[o['eng']] = j
        o = dict(eng='dve', meth='memset', kw=s.bar_kw, deps=set(last.values()), dsem=None, epoch=s.epoch)
        s.ops.append(o)
        s.bar = len(s.ops) - 1

    def op(s, eng, meth, kw, r=(), w=(), dsem=None):
        i = len(s.ops)
        deps = set()
        for t in r:
            if t in s.tok_w:
                deps.add(s.tok_w[t])
        for t in w:
            if t in s.tok_w:
                deps.add(s.tok_w[t])
            for x in s.tok_r.get(t, ()):
                deps.add(x)
        if s.bar is not None:
            deps.add(s.bar)
        o = dict(eng=eng, meth=meth, kw=kw, deps=deps, dsem=dsem, epoch=s.epoch)
        if dsem is not None:
            s.dma_cnt[dsem] = s.dma_cnt.get(dsem, 0) + 16
            o['dval'] = s.dma_cnt[dsem]
        s.ops.append(o)
        for t in r:
            s.tok_r.setdefault(t, []).append(i)
        for t in w:
            s.tok_w[t] = i
            s.tok_r[t] = []
        return i

    def emit(s):
        nc = s.nc
        needed = set()
        for o in s.ops:
            for d in o['deps']:
                p = s.ops[d]
                if p['dsem'] is None:
                    if p['eng'] == 'pe' and o['eng'] == 'pe':
                        continue
                    needed.add(d)
        cnt = {}
        for i, o in enumerate(s.ops):
            if o['dsem'] is None and i in needed:
                k = (o['eng'], o['epoch'])
                cnt[k] = cnt.get(k, 0) + 1
                o['seq'] = cnt[k]
        with ExitStack() as st:
            sems = {}
            for k in cnt:
                sems[k] = st.enter_context(nc.semaphore("s_%s_%d" % k))
            for k in s.dma_cnt:
                sems[('dma', k)] = st.enter_context(nc.semaphore("d_%s" % str(k)))
            block = st.enter_context(nc.Block())
            engobj = {'pe': block.tensor, 'act': block.scalar, 'dve': block.vector,
                      'pool': block.gpsimd, 'sp': block.sync}

            def body(ename):
                def f(e):
                    waited = {}
                    for o in s.ops:
                        if o['eng'] != ename:
                            continue
                        ws = {}
                        for d in o['deps']:
                            p = s.ops[d]
                            if p['dsem'] is not None:
                                key = ('dma', p['dsem'])
                                val = p['dval']
                            else:
                                if p['eng'] == 'pe' and ename == 'pe':
                                    continue
                                key = (p['eng'], p['epoch'])
                                val = p['seq']
                            if ws.get(key, 0) < val:
                                ws[key] = val
                        for key, val in ws.items():
                            if waited.get(key, 0) >= val:
                                continue
                            e.wait_ge(sems[key], val)
                            waited[key] = val
                        ins = getattr(e, o['meth'])(**o['kw'])
                        if o['dsem'] is not None:
                            ins.then_inc(sems[('dma', o['dsem'])], 16)
                        elif 'seq' in o:
                            ins.then_inc(sems[(o['eng'], o['epoch'])], 1)
                    if ename == 'sp':
                        for k, v in s.dma_cnt.items():
                            e.wait_ge(sems[('dma', k)], v)
                return f
            for ename in s.ENG:
                engobj[ename](body(ename))


def split_pieces(T):
    n = (T + 511) // 512
    base = ((T + n - 1) // n + 15) // 16 * 16
    out = []
    c = 0
    while c < T:
        out.append((c, min(T, c + base)))
        c += base
    return out


def build_nc(NL=4, NPCH=16, NSEQ=16, NW=3, CHAIN_RATIO=3, CH=128, CHAIN_DELAY=4):
    assert NSEQ % 4 == 0 and 4 * NSEQ <= 64
    NPT = 2 * NPCH * 64
    LS = 4 * NSEQ
    T0 = 16 + 64 * NPCH
    T1 = 64 * NPCH + LS
    TM = max(T0, T1)
    NPC = NPCH * 64 // CH
    NCHM = NPC + 1
    NCOL = NCHM + NSEQ

    nc = bass.Bass("TRN2", target_bir_lowering=False)
    dt = lambda n, s, k: nc.dram_tensor(n, s, F32, kind=k).ap()
    xp = dt("xp", [NPT, D], "ExternalInput")
    meta = dt("meta", [16, D], "ExternalInput")
    xs = dt("xs", [LS, D], "ExternalInput")
    sh = dt("sh", [NL, NSEQ, NH, 128, 128], "ExternalInput")
    sc = dt("sc", [NL, NSEQ * 2, 1024], "ExternalInput")
    w_in = dt("w_in", [NL, D, 8192], "ExternalInput")
    w_out = dt("w_out", [NL, D, D], "ExternalInput")
    conv_w = dt("conv_w", [NL * 24, 128], "ExternalInput")
    lb_param = dt("lb_param", [NL * 8, 128], "ExternalInput")
    hnw = dt("hnw", [NL, 128], "ExternalInput")
    prew = dt("prew", [NL * 16, 128], "ExternalInput")
    postw = dt("postw", [NL * 16, 128], "ExternalInput")
    yp = dt("yp", [NPT, D], "ExternalOutput")
    ys = dt("ys", [LS, D], "ExternalOutput")
    nshp = dt("nshp", [NL, NH, 128, 128], "ExternalOutput")
    ncp = dt("ncp", [NL, 2, 1024], "ExternalOutput")
    nshs = dt("nshs", [NL, NSEQ, NH, 128, 128], "ExternalOutput")
    ncs = dt("ncs", [NL, NSEQ * 2, 1024], "ExternalOutput")
    sbnd = dt("sbnd", [NL, NH, 128, 128], "Internal")
    wscr = nc.dram_tensor("wscr", [NL, 80, 128, DC * 128], BF16, kind="Internal").ap()

    st = ExitStack()
    sbt = lambda n, s, d: st.enter_context(nc.sbuf_tensor(n, s, d))
    hT = sbt("hT", [128, DC, TM], F32)
    mx = sbt("mx", [128, DC, TM], BF16)
    O_HN = 0
    W_HN = DC * TM // 2
    O_T = [W_HN + i * TM for i in range(3)]
    O_G = W_HN + 3 * TM
    O_B = O_G + TM + 2
    WB = TM // 2
    O_VT = O_B + 5 * WB
    W_VT = (NCHM + 1) * 64
    PMAX = max(c1 - c0 for T_ in (T0, T1) for (c0, c1) in split_pieces(T_))
    PSQ_ALIAS = (O_B + 3 * WB >= W_HN + DC * PMAX)
    AW = max(O_VT + 2 * W_VT, W_HN + DC * PMAX) + 2 * 256 + (0 if PSQ_ALIAS else 2 * 256)
    arena = sbt("arena", [128, AW], F32)
    hn = arena[:, O_HN:O_HN + W_HN].bitcast(BF16).rearrange("p (c t) -> p c t", t=TM)
    sqw = [arena[:, AW - 512 + i * 256: AW - 512 + (i + 1) * 256].bitcast(BF16) for i in range(2)]
    psq = None if PSQ_ALIAS else [arena[:, AW - 1024 + i * 256: AW - 1024 + (i + 1) * 256].bitcast(BF16) for i in range(2)]
    t1, t2, t3 = [arena[:, o:o + TM] for o in O_T]
    if TM >= 1024:
        g1, g2, g3 = t1, t2, t3
    else:
        g1, g2, g3 = [sbt("stg%d" % i, [128, 1024], F32)[:] for i in range(3)]
    Gb = arena[:, O_G:O_G + TM + 2]
    bfb = [arena[:, O_B + i * WB:O_B + (i + 1) * WB].bitcast(BF16) for i in range(5)]
    qt, kt, kh, gate, vT = bfb
    vtok = arena[:, O_VT:O_VT + W_VT].bitcast(BF16).rearrange("p (c v) -> p c v", v=128)
    khtok = arena[:, O_VT + W_VT:O_VT + 2 * W_VT].bitcast(BF16).rearrange("p (c v) -> p c v", v=128)
    wsl = [sbt("wsl%d" % i, [128, DC, 128], BF16) for i in range(NW)]
    smask = sbt("smask", [128, TM], BF16)
    oTb = sbt("oTb", [128, TM], F32)[:]
    onec = sbt("onec", [128, 1], F32)
    Sb = [sbt("S%d" % i, [128, 128], F32) for i in range(2)]
    Sbf = [sbt("Sbf%d" % i, [128, 128], BF16) for i in range(3)]
    Am = [sbt("Am%d" % i, [CH, CH], BF16) for i in range(2)]
    AmS = sbt("AmS", [64, 64], BF16)
    NSIN = 3
    Sin = [sbt("Sin%d" % i, [128, 4, 128], F32) for i in range(NSIN)]
    SinBf = sbt("SinBf", [128, 4, 128], BF16)
    Vblk = [sbt("Vblk%d" % i, [64, 4, 128], BF16) for i in range(2)]
    seqmask = sbt("seqmask", [64, 16], F32)
    gref = sbt("gref", [128, NCOL], F32)
    eref = sbt("eref", [128, NCOL], F32)
    bcol = sbt("bcol", [128, NCOL], F32)
    acol = sbt("acol", [128, NCOL], F32)
    us = sbt("us", [128, NSEQ, 6], F32)
    scT = sbt("scT", [128, 8, NSEQ * 2], F32)
    uprev = sbt("uprev", [128, NL, 8, 2], F32)
    ncst = sbt("ncst", [128, 8, 2], F32)
    ncsS = sbt("ncsS", [128, 8, NSEQ, 2], F32)
    identf = sbt("identf", [128, 128], F32)
    identb = sbt("identb", [128, 128], BF16)
    onesb = sbt("onesb", [128, 128], BF16)
    maskc = sbt("maskc", [CH, CH], F32)
    masks = sbt("masks", [64, 64], F32)
    preT = sbt("preT", [128, NL * 16], F32)
    postT = sbt("postT", [128, NL * 16], F32)
    lbp = sbt("lbp", [128, NL, 8], F32)
    lbT = sbt("lbT", [128, NL, 8], F32)
    omlb = sbt("omlb", [128, NL, 8], F32)
    nomlb = sbt("nomlb", [128, NL, 8], F32)
    lbtmp = sbt("lbtmp", [128, 3, 8], F32)
    nwT = sbt("nwT", [128, NL], F32)
    cwT = sbt("cwT", [128, NL * 24], F32)
    dummy = sbt("dummyt", [128, 2], F32)
    psb = [st.enter_context(nc.psum_tensor("ps%d" % i, [128, 512], F32)) for i in range(8)]
    RING = [0, 1, 2, 3, 7]
    RA, RP, RO0, RO1 = 4, 5, 6, 6
    ring_i = [0]

    def ring():
        b = RING[ring_i[0] % len(RING)]
        ring_i[0] += 1
        return b

    def PB(b):
        return ('ps', b)

    P = Prog(nc)
    P.bar_kw = dict(ap=dummy[:, 0:1], constant=0.0)
    op = P.op

    op('pool', 'memset', dict(ap=identf[:], constant=1.0), w=['identf'])
    op('pool', 'affine_select', dict(out=identf[:], in_=identf[:], pattern=[[-1, 128]], compare_op=ALU.is_equal, fill=0.0, base=0, channel_multiplier=1),
       w=['identf'])
    op('dve', 'tensor_copy', dict(out=identb[:], in_=identf[:]), r=['identf'], w=['identb'])
    op('dve', 'memset', dict(ap=onesb[:], constant=1.0), w=['onesb'])
    for i_ in range(2):
        op('dve', 'memset', dict(ap=Am[i_][:], constant=0.0), w=[('Am', i_)])
    op('dve', 'memset', dict(ap=onec[:], constant=1.0), w=['onec'])
    op('dve', 'memset', dict(ap=dummy[:], constant=0.0), w=['dummy'])
    op('dve', 'memset', dict(ap=psb[RA][:, :], constant=0.0), w=[PB(RA)])
    op('pool', 'memset', dict(ap=maskc[:], constant=1.0), w=['maskc'])
    op('pool', 'affine_select', dict(out=maskc[:], in_=maskc[:], pattern=[[1, CH]], compare_op=ALU.is_ge, fill=0.0, base=0, channel_multiplier=-1),
       w=['maskc'])
    op('pool', 'memset', dict(ap=seqmask[:], constant=1.0), w=['seqmask'])
    op('pool', 'affine_select', dict(out=seqmask[:], in_=seqmask[:], pattern=[[-4, 16]], compare_op=ALU.is_ge, fill=0.0, base=0, channel_multiplier=1), w=['seqmask'])
    op('pool', 'affine_select', dict(out=seqmask[:], in_=seqmask[:], pattern=[[4, 16]], compare_op=ALU.is_ge, fill=0.0, base=3, channel_multiplier=-1), w=['seqmask'])
    op('pool', 'memset', dict(ap=masks[:], constant=1.0), w=['masks'])
    op('pool', 'affine_select', dict(out=masks[:], in_=masks[:], pattern=[[1, 64]], compare_op=ALU.is_ge, fill=0.0, base=0, channel_multiplier=-1),
       w=['masks'])
    m3 = masks[:].rearrange("p (g t) -> p g t", t=4)
    op('pool', 'affine_select', dict(out=m3, in_=m3, pattern=[[-4, 16], [0, 4]], compare_op=ALU.is_ge, fill=0.0, base=0, channel_multiplier=1),
       w=['masks'])
    op('pool', 'affine_select', dict(out=m3, in_=m3, pattern=[[4, 16], [0, 4]], compare_op=ALU.is_ge, fill=0.0, base=3, channel_multiplier=-1),
       w=['masks'])

    def load_T(src, nrows, dst_ap, key):
        stg = g1[0:nrows, 0:128]
        op('sp', 'dma_start', dict(out=stg, in_=src), w=['t1'], dsem='io0')
        b = ring()
        op('pe', 'transpose', dict(out=psb[b][:, 0:nrows], in_=stg, identity=identf[0:nrows, 0:nrows]),
           r=['t1', 'identf'], w=[PB(b)])
        op('dve', 'tensor_copy', dict(out=dst_ap, in_=psb[b][:, 0:nrows]), w=[PB(b), key])

    load_T(prew[:, :], NL * 16, preT[:], 'preT')
    load_T(postw[:, :], NL * 16, postT[:], 'postT')
    load_T(lb_param[:, :], NL * 8, lbp[:].rearrange("p l h -> p (l h)"), 'lbp')
    load_T(hnw[:, :], NL, nwT[:], 'nwT')
    load_T(conv_w[:, :], NL * 24, cwT[:], 'cwT')
    mxl, ssum, rs = lbtmp[:, 0, :], lbtmp[:, 1, :], lbtmp[:, 2, :]
    op('dve', 'tensor_copy', dict(out=mxl, in_=lbp[:, 0, :]), r=['lbp'], w=['lbtmp'])
    for l in range(1, NL):
        op('dve', 'tensor_tensor', dict(out=mxl, in0=mxl, in1=lbp[:, l, :], op=ALU.max),
           r=['lbp'], w=['lbtmp'])
    for l in range(NL):
        op('dve', 'tensor_tensor', dict(out=lbp[:, l, :], in0=lbp[:, l, :], in1=mxl, op=ALU.subtract),
           r=['lbtmp'], w=['lbp'])
    op('act', 'activation', dict(out=lbp[:], in_=lbp[:], func=AF.Exp), w=['lbp'])
    op('dve', 'tensor_copy', dict(out=ssum, in_=lbp[:, 0, :]), r=['lbp'], w=['lbtmp'])
    for l in range(1, NL):
        op('dve', 'tensor_tensor', dict(out=ssum, in0=ssum, in1=lbp[:, l, :], op=ALU.add),
           r=['lbp'], w=['lbtmp'])
    op('dve', 'reciprocal', dict(out=rs, in_=ssum), w=['lbtmp'])
    op('dve', 'memset', dict(ap=lbT[:, 0, :], constant=0.0), w=['lbT'])
    for l in range(1, NL):
        op('dve', 'tensor_tensor', dict(out=lbp[:, l, :], in0=lbp[:, l, :], in1=rs, op=ALU.mult),
           r=['lbtmp'], w=['lbp'])
        op('dve', 'tensor_tensor', dict(out=lbT[:, l, :], in0=lbT[:, l - 1, :], in1=lbp[:, l, :], op=ALU.add),
           r=['lbp'], w=['lbT'])
    op('dve', 'tensor_scalar', dict(out=omlb[:], in0=lbT[:], scalar1=-1.0, scalar2=1.0, op0=ALU.mult, op1=ALU.add),
       r=['lbT'], w=['omlb'])
    op('dve', 'tensor_scalar', dict(out=nomlb[:], in0=lbT[:], scalar1=1.0, scalar2=-1.0, op0=ALU.mult, op1=ALU.add),
       r=['lbT'], w=['nomlb'])
    CONSTS = ['identf', 'identb', 'onesb', 'maskc', 'masks', 'preT', 'postT', 'lbT', 'omlb', 'nomlb', 'nwT', 'cwT']

    wstate = dict(n=0)

    wseen = set()

    def wload(item):
        l_, eid, src3 = item
        k = wstate['n'] % NW
        wstate['n'] += 1
        qt_ = [('w', k, c0) for c0 in range(0, DC, 4)]
        if (l_, eid) not in wseen:
            wseen.add((l_, eid))
            for c0 in range(0, DC, 4):
                op('pool', 'dma_start', dict(out=wsl[k][:, c0:c0 + 4, :], in_=src3[:, c0:c0 + 4, :]),
                   w=[('w', k, c0)], dsem='w%d_%d' % (k, c0))
            op('sp', 'dma_start', dict(out=wscr[l_, eid], in_=wsl[k][:].rearrange("p c e -> p (c e)")),
               r=qt_, w=[('wscr', l_, eid)], dsem='wst%d' % k)
        else:
            op('sp', 'dma_start', dict(out=wsl[k][:].rearrange("p c e -> p (c e)"), in_=wscr[l_, eid]),
               r=[('wscr', l_, eid)], w=qt_, dsem='wld%d' % k)
        return k

    def win_src(l, col0):
        return w_in[l].rearrange("(c p) e -> p c e", p=128)[:, :, col0:col0 + 128]

    def wout_src(l, dd):
        return w_out[l].rearrange("(c p) e -> p c e", p=128)[:, :, dd * 128:(dd + 1) * 128]

    for sb in range(2):
        if sb == 0:
            T = T0
            chunks = [(0, 16)] + [(16 + CH * i, CH) for i in range(NPC)]
            samp = None
            Tp = T0
            iot = [(meta[:, :], None, 16, 0)] + \
                  [(xp[i * 128:(i + 1) * 128, :], yp[i * 128:(i + 1) * 128, :], 128, 16 + 128 * i)
                   for i in range(NPCH // 2)]
        else:
            T = T1
            chunks = [(CH * i, CH) for i in range(NPC)]
            samp = 64 * NPCH
            Tp = 64 * NPCH
            hb = NPCH * 64
            iot = [(xp[hb + i * 128:hb + (i + 1) * 128, :], yp[hb + i * 128:hb + (i + 1) * 128, :], 128, 128 * i)
                   for i in range(NPCH // 2)] + [(xs[:, :], ys[:, :], LS, samp)]
        nch = len(chunks)
        pieces = split_pieces(T)
        pch_lo = 1 if sb == 0 else 0
        pc0 = chunks[pch_lo][0]
        npc = nch - pch_lo

        def v64(ap, lo=pc0, n=npc):
            return ap[:, lo:lo + CH * n].rearrange("p (c l) -> p c l", l=CH)

        P.epoch += 1
        op('dve', 'memset', dict(ap=smask[:], constant=1.0), w=['smask'])
        if sb == 0:
            op('dve', 'memset', dict(ap=smask[:, 0:1], constant=0.0), w=['smask'])
        op('dve', 'memset', dict(ap=v64(smask)[:, :, 0:1], constant=0.0), w=['smask'])
        if samp is not None:
            op('dve', 'memset', dict(ap=smask[:, samp:samp + LS].rearrange("p (j t) -> p j t", t=4)[:, :, 0:1], constant=0.0),
               w=['smask'])
        P.barrier()
        for ti, (src, _dst, ntok, col0) in enumerate(iot):
            for half in range(2):
                stg_full = (g2, g3)[half]
                key = ('t2', 't3')[half]
                stg = stg_full[0:ntok, 0:1024]
                op('sp', 'dma_start', dict(out=stg, in_=src[:, half * 1024:(half + 1) * 1024]),
                   r=['ARENA'], w=[key], dsem='io%d' % (1 + half))
                for q in range(2):
                    b = ring()
                    for j in range(4):
                        dcl = q * 4 + j
                        op('pe', 'transpose', dict(out=psb[b][:, j * 128:j * 128 + ntok], in_=stg[:, dcl * 128:(dcl + 1) * 128], identity=identf[0:ntok, 0:ntok]), r=[key, 'identf'], w=[PB(b)])
                    d0 = half * 8 + q * 4
                    eng = 'act' if q == 0 else 'dve'
                    srcp = psb[b][:, :].rearrange("p (j t) -> p j t", t=128)[:, :, 0:ntok]
                    dstp = hT[:, d0:d0 + 4, col0:col0 + ntok]
                    if eng == 'act':
                        op('act', 'activation', dict(out=dstp, in_=srcp, func=AF.Copy),
                           w=[PB(b)] + [('hT', d0 + j) for j in range(4)])
                    else:
                        op('dve', 'tensor_copy', dict(out=dstp, in_=srcp),
                           w=[PB(b)] + [('hT', d0 + j) for j in range(4)])

        for l in range(NL):
            P.epoch += 1
            def prenorm_gen(lp, pi):
                c0, c1 = pieces[pi]
                N = c1 - c0
                b = RO0
                for d in range(DC):
                    if PSQ_ALIAS:
                        sqb = bfb[3 + d % 2][:, 0:N]
                        sk = ('bfb', 3 + d % 2)
                    else:
                        sqb = psq[d % 2][:, 0:N]
                        sk = ('psq', d % 2)
                    op('act', 'activation', dict(out=sqb, in_=hT[:, d, c0:c1], func=AF.Square),
                       r=[('hT', d)], w=[sk])
                    yield
                    op('pe', 'matmul', dict(out=psb[b][:, 0:N], lhsT=onesb[:], rhs=sqb, start=(d == 0), stop=(d == DC - 1)),
                       r=[sk, 'onesb'], w=[PB(b)])
                op('dve', 'tensor_scalar', dict(out=oTb[:, c0:c1], in0=psb[b][:, 0:N], scalar1=1.0 / D, scalar2=EPS, op0=ALU.mult, op1=ALU.add),
                   w=[PB(b), 'oT'])
                op('act', 'activation', dict(out=oTb[:, c0:c1], in_=oTb[:, c0:c1], func=AF.Ln), w=['oT'])
                op('act', 'activation', dict(out=oTb[:, c0:c1], in_=oTb[:, c0:c1], func=AF.Exp, scale=-0.5), w=['oT'])
                for d in range(DC):
                    op('dve', 'scalar_tensor_tensor', dict(out=hn[:, d, c0:c1], in0=hT[:, d, c0:c1], scalar=preT[:, lp * 16 + d:lp * 16 + d + 1], in1=oTb[:, c0:c1], op0=ALU.mult, op1=ALU.mult),
                       r=[('hT', d), 'oT', 'preT'], w=[('hn', d, pi)])

            if l == 0:
                P.barrier()
                for pi in range(len(pieces)):
                    for _ in prenorm_gen(0, pi):
                        pass

            def proj(slot):
                res = []
                for (c0, c1) in pieces:
                    b = ring()
                    N = c1 - c0
                    for d in range(DC):
                        op('pe', 'matmul', dict(out=psb[b][:, 0:N], lhsT=wsl[slot][:, d, :], rhs=hn[:, d, c0:c1], start=(d == 0), stop=(d == DC - 1)),
                           r=[('w', slot, (d // 4) * 4), ('hn', d)], w=[PB(b)])
                    res.append((b, c0, c1))
                return res

            wq = []
            for h in range(NH):
                for qi in (1, 2, 3, 0):
                    wq.append((l, qi * 8 + h, win_src(l, qi * 1024 + h * 128)))
                for qi in (5, 6, 4, 7):
                    wq.append((l, qi * 8 + h, win_src(l, qi * 1024 + h * 128)))
            for pi_ in range(len(pieces)):
                for dd in range(DC):
                    wq.append((l, 64 + dd, wout_src(l, dd)))
            wpos = dict(i=0, slots=[])

            def wnext():
                while len(wpos['slots']) < NW - 1 + 1 and wpos['i'] < len(wq):
                    wpos['slots'].append(wload(wq[wpos['i']]))
                    wpos['i'] += 1
                return wpos['slots'].pop(0)

            if samp is not None:
                stg = g1[0:NSEQ * 2, 0:1024]
                op('sp', 'dma_start', dict(out=stg, in_=sc[l]), r=['ARENA'], w=['t1'], dsem='io0')
                b = ring()
                for cb in range(8):
                    op('pe', 'transpose', dict(out=psb[b][:, cb * NSEQ * 2:(cb + 1) * NSEQ * 2], in_=stg[:, cb * 128:(cb + 1) * 128], identity=identf[0:NSEQ * 2, 0:NSEQ * 2]), r=['t1', 'identf'], w=[PB(b)])
                op('dve', 'tensor_copy', dict(out=scT[:].rearrange("p c n -> p (c n)"), in_=psb[b][:, 0:8 * NSEQ * 2]),
                   w=[PB(b), 'scT'])

            def proj_gen(slot):
                for pi_, (c0, c1) in enumerate(pieces):
                    b = ring()
                    N = c1 - c0
                    for d in range(DC):
                        op('pe', 'matmul', dict(out=psb[b][:, 0:N], lhsT=wsl[slot][:, d, :], rhs=hn[:, d, c0:c1], start=(d == 0), stop=(d == DC - 1)),
                           r=[('w', slot, (d // 4) * 4), ('hn', d, pi_)], w=[PB(b)])
                    yield (b, c0, c1)

            tchunks = list(chunks) + ([(samp, LS)] if samp is not None else [])

            def tok_major(srcb, srck, dst, dstk, eng):
                for g0 in range(0, len(tchunks), 8):
                    grp = tchunks[g0:g0 + 8]
                    b = ring()
                    pbv = psb[b][:, :].bitcast(BF16)
                    for j, (cc0, L) in enumerate(grp):
                        op('pe', 'transpose', dict(out=pbv[0:L, j * 128:(j + 1) * 128], in_=srcb[:, cc0:cc0 + L], identity=identb[:]),
                           r=[srck, 'identb'], w=[PB(b)])
                    j = 0
                    while j < len(grp):
                        L = grp[j][1]
                        j1 = j
                        while j1 < len(grp) and grp[j1][1] == L:
                            j1 += 1
                        srcp = pbv[0:L, j * 128:j1 * 128].rearrange("p (c v) -> p c v", v=128)
                        dstp = dst[0:L, g0 + j:g0 + j1, :]
                        if eng == 'act':
                            op('act', 'activation', dict(out=dstp, in_=srcp, func=AF.Copy), w=[PB(b), dstk])
                        else:
                            op('dve', 'tensor_copy', dict(out=dstp, in_=srcp), w=[PB(b), dstk])
                        j = j1

            def front(h, side=None):
                lbc = lbT[:, l, h:h + 1]
                omc = omlb[:, l, h:h + 1]
                nomc = nomlb[:, l, h:h + 1]
                sl = lambda ap: ap[:, 0:T]
                slot = wnext()
                for (b, c0, c1) in proj_gen(slot):
                    op('act', 'activation', dict(out=t1[:, c0:c1], in_=psb[b][:, 0:c1 - c0], func=AF.Exp, scale=-1.0),
                       w=[PB(b), 't1'])
                    if side is not None:
                        next(side, None)
                if side is not None:
                    for _ in side:
                        pass
                op('dve', 'tensor_scalar', dict(out=sl(t1), in0=sl(t1), scalar1=1e30, scalar2=1.0, op0=ALU.min, op1=ALU.add), w=['t1'])
                op('act', 'activation', dict(out=sl(t1), in_=sl(t1), func=AF.Ln), w=['t1'])
                op('act', 'activation', dict(out=sl(t1), in_=sl(t1), func=AF.Exp, scale=-1.0), w=['t1'])
                slot = wnext()
                for (b, c0, c1) in proj_gen(slot):
                    op('act', 'activation', dict(out=vT[:, c0:c1], in_=psb[b][:, 0:c1 - c0], func=AF.Copy),
                       w=[PB(b), ('bfb', 4)])
                op('dve', 'tensor_scalar', dict(out=sl(t2), in0=sl(t1), scalar1=omc, scalar2=lbc, op0=ALU.mult, op1=ALU.add),
                   r=['t1', 'omlb', 'lbT'], w=['t2'])
                op('act', 'activation', dict(out=sl(t2), in_=sl(t2), func=AF.Ln), w=['t2'])
                op('dve', 'tensor_scalar', dict(out=sl(t1), in0=sl(t1), scalar1=nomc, scalar2=omc, op0=ALU.mult, op1=ALU.add),
                   r=['nomlb', 'omlb'], w=['t1'])
                slot = wnext()
                for (b, c0, c1) in proj_gen(slot):
                    N = c1 - c0
                    op('act', 'activation', dict(out=oTb[:, c0:c1], in_=psb[b][:, 0:N], func=AF.Exp, scale=-1.0),
                       w=[PB(b), 'oT'])
                    op('act', 'activation', dict(out=oTb[:, c0:c1], in_=oTb[:, c0:c1], func=AF.Ln, bias=onec[:, 0:1]), r=['onec'], w=['oT'])
                    op('act', 'activation', dict(out=oTb[:, c0:c1], in_=oTb[:, c0:c1], func=AF.Exp, scale=-1.0), w=['oT'])
                    op('dve', 'tensor_tensor', dict(out=gate[:, c0:c1], in0=psb[b][:, 0:N], in1=oTb[:, c0:c1], op=ALU.mult),
                       r=['oT'], w=[PB(b), ('bfb', 3)])
                Gv = Gb[:, 0:T]
                op('dve', 'tensor_tensor_scan', dict(out=Gv, data0=smask[:, 0:T], data1=sl(t2), initial=0.0, op0=ALU.mult, op1=ALU.add),
                   r=['t2', 'smask'], w=['G'])
                if sb == 0:
                    op('dve', 'tensor_copy', dict(out=gref[:, 0:1], in_=Gb[:, 7:8]), r=['G'], w=['gref'])
                op('dve', 'tensor_copy', dict(out=gref[:, pch_lo:nch], in_=v64(Gb)[:, :, CH // 2 - 1]), r=['G'], w=['gref'])
                if sb == 0:
                    op('dve', 'tensor_scalar', dict(out=Gb[:, 0:16], in0=Gb[:, 0:16], scalar1=gref[:, 0:1], scalar2=None, op0=ALU.subtract),
                       r=['gref'], w=['G'])
                op('dve', 'tensor_tensor', dict(out=v64(Gb), in0=v64(Gb), in1=gref[:, pch_lo:nch].unsqueeze(2).to_broadcast([128, npc, CH]), op=ALU.subtract),
                   r=['gref'], w=['G'])
                if sb == 0:
                    op('dve', 'tensor_tensor', dict(out=bcol[:, 0:1], in0=Gb[:, 15:16], in1=gref[:, 0:1], op=ALU.add), r=['G', 'gref'], w=['bcol'])
                op('dve', 'tensor_tensor', dict(out=bcol[:, pch_lo:nch], in0=v64(Gb)[:, :, CH - 1], in1=gref[:, pch_lo:nch], op=ALU.add),
                   r=['G', 'gref'], w=['bcol'])
                ncol_used = nch
                if samp is not None:
                    op('dve', 'tensor_copy', dict(out=bcol[:, nch:nch + NSEQ], in_=Gb[:, samp:samp + LS].rearrange("p (j t) -> p j t", t=4)[:, :, 3]),
                       r=['G'], w=['bcol'])
                    ncol_used = nch + NSEQ
                op('act', 'activation', dict(out=bcol[:, 0:ncol_used], in_=bcol[:, 0:ncol_used], func=AF.Exp), w=['bcol'])
                op('act', 'activation', dict(out=eref[:, 0:nch], in_=gref[:, 0:nch], func=AF.Exp), r=['gref'], w=['eref'])
                op('act', 'activation', dict(out=sl(t2), in_=Gv, func=AF.Exp), r=['G'], w=['t2'])
                op('act', 'activation', dict(out=sl(t3), in_=Gv, func=AF.Exp, scale=-1.0), r=['G'], w=['t3'])
                if sb == 0:
                    op('dve', 'tensor_copy', dict(out=acol[:, 0:1], in_=t2[:, 15:16]), r=['t2'], w=['acol'])
                op('dve', 'tensor_copy', dict(out=acol[:, pch_lo:nch], in_=v64(t2)[:, :, CH - 1]), r=['t2'], w=['acol'])
                if samp is not None:
                    op('dve', 'tensor_copy', dict(out=acol[:, nch:nch + NSEQ], in_=t2[:, samp:samp + LS].rearrange("p (j t) -> p j t", t=4)[:, :, 3]),
                       r=['t2'], w=['acol'])
                tok_major(vT, ('bfb', 4), vtok, 'vtok', 'act')
                slot = wnext()
                for (b, c0, c1) in proj_gen(slot):
                    op('dve', 'tensor_tensor', dict(out=qt[:, c0:c1], in0=psb[b][:, 0:c1 - c0], in1=t2[:, c0:c1], op=ALU.mult),
                       r=['t2'], w=[PB(b), ('bfb', 0)])
                op('dve', 'tensor_tensor', dict(out=sl(kt), in0=sl(t1), in1=sl(t3), op=ALU.mult), r=['t1', 't3'], w=[('bfb', 1)])
                if sb == 0:
                    op('dve', 'tensor_scalar', dict(out=kh[:, 0:16], in0=kt[:, 0:16], scalar1=acol[:, 0:1], scalar2=None, op0=ALU.mult),
                       r=[('bfb', 1), 'acol'], w=[('bfb', 2)])
                op('dve', 'tensor_tensor', dict(out=v64(kh), in0=v64(kt), in1=acol[:, pch_lo:nch].unsqueeze(2).to_broadcast([128, npc, CH]), op=ALU.mult),
                   r=[('bfb', 1), 'acol'], w=[('bfb', 2)])
                if samp is not None:
                    s4 = lambda ap: ap[:, samp:samp + LS].rearrange("p (j t) -> p j t", t=4)
                    op('dve', 'tensor_tensor', dict(out=s4(kh), in0=s4(kt), in1=acol[:, nch:nch + NSEQ].unsqueeze(2).to_broadcast([128, NSEQ, 4]), op=ALU.mult),
                       r=[('bfb', 1), 'acol'], w=[('bfb', 2)])

            def sin_load(h, qd):
                j0 = qd * 4
                si = (h * (NSEQ // 4) + qd) % NSIN
                op('sp', 'dma_start', dict(out=Sin[si][:], in_=sh[l, j0:j0 + 4, h].rearrange("j k v -> k j v")),
                   w=[('Sin', si)], dsem='Sin%d' % si)

            def vblk_build(qd):
                j0 = qd * 4
                op('pool', 'tensor_tensor', dict(out=Vblk[qd % 2][0:LS], in0=vtok[0:LS, nch, :].unsqueeze(1).to_broadcast([LS, 4, 128]),
                                                in1=seqmask[0:LS, j0:j0 + 4].unsqueeze(2).to_broadcast([LS, 4, 128]), op=ALU.mult),
                   r=['vtok', 'seqmask'], w=[('Vblk', qd % 2)])

            def chain_gen(h):
                oT = oTb
                if samp is not None:
                    sin_load(h, 0)
                    sin_load(h, 1)
                for _ in range(CHAIN_DELAY):
                    yield
                tok_major(kh, ('bfb', 2), khtok, 'khtok', 'dve')
                if samp is not None:
                    vblk_build(0)
                    if NSEQ // 4 > 1:
                        vblk_build(1)
                if sb == 0:
                    op('dve', 'memset', dict(ap=Sb[0][:], constant=0.0), w=[('S', 0)])
                else:
                    op('sp', 'dma_start', dict(out=Sb[0][:], in_=sbnd[l, h]), r=['sbnd%d_%d' % (l, h)], w=[('S', 0)], dsem='S0')
                op('act', 'activation', dict(out=Sbf[0][:], in_=Sb[0][:], func=AF.Identity, scale=eref[:, 0:1]),
                   r=[('S', 0), 'eref'], w=[('Sbf', 0)])
                ob_i = 0
                win = None
                for ci in range(nch + 1):
                    if ci < nch:
                        cc0, L = chunks[ci]
                        a_i = ci % 2
                        n_i = (ci + 1) % 2
                        if L > 64:
                            Hh = L // 2
                            op('pe', 'matmul', dict(out=psb[RA][0:L, Hh:L], lhsT=kt[:, cc0:cc0 + L], rhs=qt[:, cc0 + Hh:cc0 + L], start=True, stop=True),
                               r=[('bfb', 1), ('bfb', 0)], w=[PB(RA)])
                            op('pe', 'matmul', dict(out=psb[RA][0:Hh, 0:Hh], lhsT=kt[:, cc0:cc0 + Hh], rhs=qt[:, cc0:cc0 + Hh], start=True, stop=True),
                               r=[('bfb', 1), ('bfb', 0)], w=[PB(RA)])
                        else:
                            op('pe', 'matmul', dict(out=psb[RA][0:L, 0:L], lhsT=kt[:, cc0:cc0 + L], rhs=qt[:, cc0:cc0 + L], start=True, stop=True),
                               r=[('bfb', 1), ('bfb', 0)], w=[PB(RA)])
                        op('pe', 'matmul', dict(out=psb[RP][:, 0:128], lhsT=khtok[0:L, ci, :], rhs=vtok[0:L, ci, :], start=True, stop=True),
                           r=['khtok', 'vtok'], w=[PB(RP)])
                        op('dve', 'copy_predicated', dict(out=Am[a_i][0:L, 0:L], mask=maskc[0:L, 0:L].bitcast(mybir.dt.uint32), data=psb[RA][0:L, 0:L]),
                           r=['maskc'], w=[PB(RA), ('Am', a_i)])
                        op('dve', 'scalar_tensor_tensor', dict(out=Sb[n_i][:], in0=Sb[ci % 2][:], scalar=bcol[:, ci:ci + 1], in1=psb[RP][:, 0:128], op0=ALU.mult, op1=ALU.add),
                           r=[('S', ci % 2), 'bcol'], w=[PB(RP), ('S', n_i)])
                        if ci + 1 < nch:
                            op('act', 'activation', dict(out=Sbf[(ci + 1) % 3][:], in_=Sb[n_i][:], func=AF.Identity, scale=eref[:, ci + 1:ci + 2]),
                               r=[('S', n_i), 'eref'], w=[('Sbf', (ci + 1) % 3)])
                    if ci >= 1:
                        cj = ci - 1
                        cc0, L = chunks[cj]
                        if win is None:
                            win = [(RO0, RO1)[ob_i % 2], cc0]
                            ob_i += 1
                        ob, w0 = win
                        oc = cc0 - w0
                        op('pe', 'matmul', dict(out=psb[ob][:, oc:oc + L], lhsT=vtok[0:L, cj, :], rhs=Am[cj % 2][0:L, 0:L], start=True, stop=False),
                           r=['vtok', ('Am', cj % 2)], w=[PB(ob)])
                        op('pe', 'matmul', dict(out=psb[ob][:, oc:oc + L], lhsT=Sbf[cj % 3][:], rhs=qt[:, cc0:cc0 + L], start=False, stop=True),
                           r=[('Sbf', cj % 3), ('bfb', 0)], w=[PB(ob)])
                        nxt_end = (chunks[cj + 1][0] + chunks[cj + 1][1] - w0) if cj + 1 < nch else None
                        if nxt_end is None or nxt_end > 512:
                            wlen = cc0 + L - w0
                            op('act', 'activation', dict(out=oT[:, w0:w0 + wlen], in_=psb[ob][:, 0:wlen], func=AF.Copy),
                               w=[PB(ob), 'oT'])
                            win = None
                    yield
                fin = nch % 2
                if sb == 0:
                    op('sp', 'dma_start', dict(out=sbnd[l, h], in_=Sb[fin][:]), r=[('S', fin)], w=['sbnd%d_%d' % (l, h)], dsem='Sst')
                else:
                    op('sp', 'dma_start', dict(out=nshp[l, h], in_=Sb[fin][:]), r=[('S', fin)], dsem='Sst')
                if samp is not None:
                    ci_s = nch
                    op('pe', 'matmul', dict(out=psb[RA][0:LS, 0:LS], lhsT=kt[:, samp:samp + LS], rhs=qt[:, samp:samp + LS], start=True, stop=True),
                       r=[('bfb', 1), ('bfb', 0)], w=[PB(RA)])
                    op('dve', 'tensor_tensor', dict(out=AmS[0:LS, 0:LS], in0=psb[RA][0:LS, 0:LS], in1=masks[0:LS, 0:LS], op=ALU.mult),
                       r=['masks'], w=[PB(RA), 'AmS'])
                    ob = (RO0, RO1)[ob_i % 2]
                    ob_i += 1
                    op('pe', 'matmul', dict(out=psb[ob][:, 0:LS], lhsT=vtok[0:LS, ci_s, :], rhs=AmS[0:LS, 0:LS], start=True, stop=False),
                       r=['vtok', 'AmS'], w=[PB(ob)])
                    yield
                    NQ = NSEQ // 4
                    for qd in range(NQ):
                        j0 = qd * 4
                        si = (h * NQ + qd) % NSIN
                        sk = ('Sin', si)
                        Sq = Sin[si]
                        if qd + 2 < NQ:
                            sin_load(h, qd + 2)
                        op('act', 'activation', dict(out=SinBf[:], in_=Sq[:], func=AF.Copy), r=[sk], w=['SinBf'])
                        Vb = Vblk[qd % 2]
                        vk = ('Vblk', qd % 2)
                        yield
                        for jq in range(4):
                            j = j0 + jq
                            last = (j == NSEQ - 1)
                            op('pe', 'matmul', dict(out=psb[ob][:, 4 * j:4 * j + 4], lhsT=SinBf[:, jq, :], rhs=qt[:, samp + 4 * j:samp + 4 * j + 4], start=False, stop=last),
                               r=['SinBf', ('bfb', 0)], w=[PB(ob)])
                        op('pe', 'matmul', dict(out=psb[RP][:, 0:512], lhsT=khtok[0:LS, ci_s, :], rhs=Vb[0:LS].rearrange("p j v -> p (j v)"), start=True, stop=True),
                           r=['khtok', vk], w=[PB(RP)])
                        if qd + 2 < NQ:
                            vblk_build(qd + 2)
                        for jq in range(4):
                            j = j0 + jq
                            op('dve', 'scalar_tensor_tensor', dict(out=Sq[:, jq, :], in0=Sq[:, jq, :], scalar=bcol[:, nch + j:nch + j + 1], in1=psb[RP][:, jq * 128:(jq + 1) * 128], op0=ALU.mult, op1=ALU.add),
                               r=['bcol'], w=[PB(RP), sk])
                        op('sp', 'dma_start', dict(out=nshs[l, j0:j0 + 4, h].rearrange("j k v -> k j v"), in_=Sq[:]),
                           r=[sk], dsem='So%d' % si)
                        yield
                    op('act', 'activation', dict(out=oT[:, samp:samp + LS], in_=psb[ob][:, 0:LS], func=AF.Copy),
                       w=[PB(ob), 'oT'])
                    yield

            def norm_gen(h):
                oT = oTb
                for (c0, c1) in pieces:
                    op('act', 'activation', dict(out=kt[:, c0:c1], in_=oT[:, c0:c1], func=AF.Square),
                       r=['oT'], w=[('bfb', 1)])
                yield
                banks = []
                for (c0, c1) in pieces:
                    N = c1 - c0
                    b = ring()
                    banks.append(b)
                    op('pe', 'matmul', dict(out=psb[b][:, 0:N], lhsT=onesb[:], rhs=kt[:, c0:c1], start=True, stop=True),
                       r=[('bfb', 1), 'onesb'], w=[PB(b)])
                yield
                for b, (c0, c1) in zip(banks, pieces):
                    N = c1 - c0
                    op('dve', 'tensor_scalar', dict(out=t3[:, c0:c1], in0=psb[b][:, 0:N], scalar1=1.0 / 128, scalar2=EPS, op0=ALU.mult, op1=ALU.add),
                       w=[PB(b), 't3'])
                    op('act', 'activation', dict(out=t3[:, c0:c1], in_=t3[:, c0:c1], func=AF.Ln), w=['t3'])
                    op('act', 'activation', dict(out=t3[:, c0:c1], in_=t3[:, c0:c1], func=AF.Exp, scale=-0.5), w=['t3'])
                    op('dve', 'scalar_tensor_tensor', dict(out=oT[:, c0:c1], in0=oT[:, c0:c1], scalar=nwT[:, l:l + 1], in1=t3[:, c0:c1], op0=ALU.mult, op1=ALU.mult),
                       r=['t3', 'nwT'], w=['oT'])
                    op('dve', 'tensor_tensor', dict(out=mx[:, h, c0:c1], in0=oT[:, c0:c1], in1=gate[:, c0:c1], op=ALU.mult),
                       r=['oT', ('bfb', 3)], w=[('mx', h)])

            def cb_gen(cb):
                w0c = cwT[:, l * 24 + 0 * 8 + cb:l * 24 + 0 * 8 + cb + 1]
                w1c = cwT[:, l * 24 + 1 * 8 + cb:l * 24 + 1 * 8 + cb + 1]
                w2c = cwT[:, l * 24 + 2 * 8 + cb:l * 24 + 2 * 8 + cb + 1]
                ub = Gb
                slot = wnext()
                for (b, c0, c1) in proj_gen(slot):
                    op('act', 'activation', dict(out=t1[:, c0:c1], in_=psb[b][:, 0:c1 - c0], func=AF.Copy),
                       w=[PB(b), 't1'])
                    yield
                slot = wnext()
                for (b, c0, c1) in proj_gen(slot):
                    op('dve', 'tensor_tensor', dict(out=ub[:, 2 + c0:2 + c1], in0=psb[b][:, 0:c1 - c0], in1=t1[:, c0:c1], op=ALU.mult),
                       r=['t1'], w=[PB(b), 'G'])
                    yield
                if sb == 0:
                    op('dve', 'memset', dict(ap=ub[:, 0:2], constant=0.0), w=['G'])
                else:
                    op('dve', 'tensor_copy', dict(out=ub[:, 0:2], in_=uprev[:, l, cb, :]), r=['uprev'], w=['G'])
                op('dve', 'tensor_scalar', dict(out=t2[:, 0:Tp], in0=ub[:, 2:2 + Tp], scalar1=w2c, scalar2=None, op0=ALU.mult),
                   r=['G', 'cwT'], w=['t2'])
                op('dve', 'scalar_tensor_tensor', dict(out=t2[:, 0:Tp], in0=ub[:, 1:1 + Tp], scalar=w1c, in1=t2[:, 0:Tp], op0=ALU.mult, op1=ALU.add),
                   r=['G', 'cwT'], w=['t2'])
                op('dve', 'scalar_tensor_tensor', dict(out=t2[:, 0:Tp], in0=ub[:, 0:Tp], scalar=w0c, in1=t2[:, 0:Tp], op0=ALU.mult, op1=ALU.add),
                   r=['G', 'cwT'], w=['t2'])
                if sb == 0:
                    op('dve', 'tensor_copy', dict(out=uprev[:, l, cb, :], in_=ub[:, Tp:Tp + 2]), r=['G'], w=['uprev'])
                else:
                    op('dve', 'tensor_copy', dict(out=ncst[:, cb, :], in_=ub[:, Tp:Tp + 2]), r=['G'], w=['ncst'])
                if samp is not None:
                    op('dve', 'tensor_copy', dict(out=us[:, :, 0:2], in_=scT[:, cb, :].rearrange("p (j r) -> p j r", r=2)),
                       r=['scT'], w=['us'])
                    op('dve', 'tensor_copy', dict(out=us[:, :, 2:6], in_=ub[:, 2 + samp:2 + samp + LS].rearrange("p (j t) -> p j t", t=4)),
                       r=['G'], w=['us'])
                    t2s = t2[:, samp:samp + LS].rearrange("p (j t) -> p j t", t=4)
                    op('dve', 'tensor_scalar', dict(out=t2s, in0=us[:, :, 2:6], scalar1=w2c, scalar2=None, op0=ALU.mult),
                       r=['us', 'cwT'], w=['t2'])
                    op('dve', 'scalar_tensor_tensor', dict(out=t2s, in0=us[:, :, 1:5], scalar=w1c, in1=t2s, op0=ALU.mult, op1=ALU.add),
                       r=['us', 'cwT'], w=['t2'])
                    op('dve', 'scalar_tensor_tensor', dict(out=t2s, in0=us[:, :, 0:4], scalar=w0c, in1=t2s, op0=ALU.mult, op1=ALU.add),
                       r=['us', 'cwT'], w=['t2'])
                    op('dve', 'tensor_copy', dict(out=ncsS[:, cb, :, :], in_=us[:, :, 4:6]), r=['us'], w=['ncsS'])
                slot = wnext()
                for (b, c0, c1) in proj_gen(slot):
                    op('dve', 'tensor_tensor', dict(out=t2[:, c0:c1], in0=psb[b][:, 0:c1 - c0], in1=t2[:, c0:c1], op=ALU.mult),
                       w=[PB(b), 't2'])
                    yield
                slot = wnext()
                for (b, c0, c1) in proj_gen(slot):
                    N = c1 - c0
                    op('act', 'activation', dict(out=t3[:, c0:c1], in_=psb[b][:, 0:N], func=AF.Exp, scale=-1.0),
                       w=[PB(b), 't3'])
                    op('act', 'activation', dict(out=t3[:, c0:c1], in_=t3[:, c0:c1], func=AF.Ln, bias=onec[:, 0:1]), r=['onec'], w=['t3'])
                    op('act', 'activation', dict(out=t3[:, c0:c1], in_=t3[:, c0:c1], func=AF.Exp, scale=-1.0), w=['t3'])
                    op('dve', 'tensor_tensor', dict(out=t3[:, c0:c1], in0=psb[b][:, 0:N], in1=t3[:, c0:c1], op=ALU.mult),
                       w=[PB(b), 't3'])
                    op('dve', 'tensor_tensor', dict(out=mx[:, 8 + cb, c0:c1], in0=t2[:, c0:c1], in1=t3[:, c0:c1], op=ALU.mult),
                       r=['t2', 't3'], w=[('mx', 8 + cb)])
                    yield

            def merge(ga, gb, ra):
                da = db = False
                while not (da and db):
                    if not db:
                        try:
                            next(gb)
                        except StopIteration:
                            db = True
                    for _ in range(ra):
                        if not da:
                            try:
                                next(ga)
                            except StopIteration:
                                da = True

            side = None
            for i in range(NH):
                front(i, side)
                merge(chain_gen(i), cb_gen(i), CHAIN_RATIO)
                side = norm_gen(i)
            for _ in side:
                pass
            if sb == 1:
                stg = g1[0:2, 0:1024]
                for g in range(2):
                    b = ring()
                    for j in range(4):
                        cb = g * 4 + j
                        op('pe', 'transpose', dict(out=psb[b][0:2, j * 128:(j + 1) * 128], in_=ncst[:, cb, :], identity=identf[:]),
                           r=['ncst', 'identf'], w=[PB(b)])
                    op('dve', 'tensor_copy', dict(out=stg[:, g * 512:(g + 1) * 512], in_=psb[b][0:2, 0:512]),
                       r=['ARENA'], w=[PB(b), 't1'])
                op('sp', 'dma_start', dict(out=ncp[l], in_=stg), r=['t1', 'ARENA'], dsem='io0')
                stg2 = g3[0:NSEQ * 2, 0:1024]
                for g in range(2):
                    b = ring()
                    for j in range(4):
                        cb = g * 4 + j
                        op('pe', 'transpose', dict(out=psb[b][0:NSEQ * 2, j * 128:(j + 1) * 128], in_=ncsS[:, cb, :, :].rearrange("p j r -> p (j r)"), identity=identf[:]),
                           r=['ncsS', 'identf'], w=[PB(b)])
                    op('dve', 'tensor_copy', dict(out=stg2[:, g * 512:(g + 1) * 512], in_=psb[b][0:NSEQ * 2, 0:512]),
                       r=['ARENA'], w=[PB(b), 't3'])
                op('sp', 'dma_start', dict(out=ncs[l], in_=stg2), r=['t3', 'ARENA'], dsem='io3')

            mo_p = arena[:, W_HN:W_HN + DC * PMAX].rearrange("p (c t) -> p c t", t=PMAX)
            MO_TOK = ['t1', 't2', 't3', 'G'] + [('bfb', i_) for i_ in range(5)]
            pend_ones = []

            def flush_ones(keep):
                while len(pend_ones) > keep:
                    kw = pend_ones.pop(0)
                    r_ = kw.pop('_r')
                    w_ = kw.pop('_w')
                    op('pe', 'matmul', kw, r=r_, w=w_)

            side = None
            for pi, (c0, c1) in enumerate(pieces):
                N = c1 - c0
                SQ = (RA, RP)[pi % 2]
                for dd in range(DC):
                    slot = wnext()
                    b = ring()
                    for ec in range(DC):
                        if ec == DC // 2:
                            flush_ones(0)
                        op('pe', 'matmul', dict(out=psb[b][:, 0:N], lhsT=wsl[slot][:, ec, :], rhs=mx[:, ec, c0:c1], start=(ec == 0), stop=(ec == DC - 1)),
                           r=[('w', slot, (ec // 4) * 4), ('mx', ec)], w=[PB(b)])
                    if side is not None:
                        next(side, None)
                    sq = sqw[dd % 2][:, 0:N]
                    sqk = ('sqw', dd % 2)
                    op('act', 'activation', dict(out=sq, in_=psb[b][:, 0:N], func=AF.Square),
                       w=[PB(b), sqk])
                    op('dve', 'tensor_copy', dict(out=mo_p[:, dd, 0:N], in_=psb[b][:, 0:N]),
                       w=[PB(b), ('mo', dd)] + (MO_TOK if (pi == 0 and dd == 0) else []))
                    pend_ones.append(dict(out=psb[SQ][:, 0:N], lhsT=onesb[:], rhs=sq, start=(dd == 0), stop=(dd == DC - 1), _r=[sqk, 'onesb'], _w=[PB(SQ)]))
                flush_ones(0)
                if side is not None:
                    for _ in side:
                        pass
                    side = None
                op('dve', 'tensor_scalar', dict(out=oTb[:, c0:c1], in0=psb[SQ][:, 0:N], scalar1=1.0 / D, scalar2=EPS, op0=ALU.mult, op1=ALU.add),
                   w=[PB(SQ), 'oT'])
                op('act', 'activation', dict(out=oTb[:, c0:c1], in_=oTb[:, c0:c1], func=AF.Ln), w=['oT'])
                op('act', 'activation', dict(out=oTb[:, c0:c1], in_=oTb[:, c0:c1], func=AF.Exp, scale=-0.5), w=['oT'])
                for dd in range(DC):
                    op('dve', 'scalar_tensor_tensor', dict(out=mo_p[:, dd, 0:N], in0=mo_p[:, dd, 0:N], scalar=postT[:, l * 16 + dd:l * 16 + dd + 1], in1=oTb[:, c0:c1], op0=ALU.mult, op1=ALU.mult),
                       r=['oT', 'postT'], w=[('mo', dd)])
                    op('pool', 'tensor_tensor', dict(out=hT[:, dd, c0:c1], in0=hT[:, dd, c0:c1], in1=mo_p[:, dd, 0:N], op=ALU.add),
                       r=[('mo', dd)], w=[('hT', dd)])
                if l + 1 < NL:
                    side = prenorm_gen(l + 1, pi)
            if side is not None:
                for _ in side:
                    pass
            for dd in range(DC):
                P.tok_r.setdefault('t1', [])
            for t_ in MO_TOK:
                lst = P.tok_r.setdefault(t_, [])
                for dd in range(DC):
                    if ('mo', dd) in P.tok_w:
                        lst.append(P.tok_w[('mo', dd)])
                    lst.extend(P.tok_r.get(('mo', dd), []))

        P.barrier()
        sti = 0
        for (src, dst, ntok, col0) in iot:
            if dst is None:
                continue
            for q in range(4):
                b = ring()
                for j in range(4):
                    d = q * 4 + j
                    op('pe', 'transpose', dict(out=psb[b][0:ntok, j * 128:(j + 1) * 128], in_=hT[:, d, col0:col0 + ntok], identity=identf[:]),
                       r=[('hT', d), 'identf'], w=[PB(b)])
                stg_full = (g2, g3)[sti % 2]
                key = ('t2', 't3')[sti % 2]
                stg = stg_full[0:ntok, 0:512]
                if sti % 2 == 0:
                    op('act', 'activation', dict(out=stg, in_=psb[b][0:ntok, 0:512], func=AF.Copy),
                       r=['ARENA'], w=[PB(b), key])
                else:
                    op('dve', 'tensor_copy', dict(out=stg, in_=psb[b][0:ntok, 0:512]),
                       r=['ARENA'], w=[PB(b), key])
                op('sp', 'dma_start', dict(out=dst[:, q * 512:(q + 1) * 512], in_=stg),
                   r=[key, 'ARENA'], dsem='io%d' % (1 + sti % 2))
                sti += 1

    P.emit()
    st.close()
    return nc


_NC_CACHE = {}


def kernel(x_prompt, x_sample, state_hgrn, state_conv, meta_tokens, w_in, conv_w, lb_param,
           hgrn_norm_w, w_out, pre_norm_w, post_norm_w):
    NL = w_in.shape[0]
    B, SEQ, _ = x_prompt.shape
    DECB = x_sample.shape[0]
    NSEQ = DECB // N_CORES
    NPCH = SEQ // 128
    key = (NL, NPCH, NSEQ)
    if key not in _NC_CACHE:
        _NC_CACHE[key] = build_nc(NL=NL, NPCH=NPCH, NSEQ=NSEQ)
    nc = _NC_CACHE[key]
    f = lambda a: np.ascontiguousarray(np.asarray(a, dtype=np.float32))
    common = dict(
        meta=f(meta_tokens), w_in=f(w_in), w_out=f(w_out),
        conv_w=f(conv_w).reshape(NL * 24, 128), lb_param=f(lb_param).reshape(NL * 8, 128),
        hnw=f(hgrn_norm_w), prew=f(pre_norm_w).reshape(NL * 16, 128), postw=f(post_norm_w).reshape(NL * 16, 128))
    xs_all = f(x_sample)
    sh_all = f(state_hgrn)
    sc_all = f(state_conv)
    xp_all = f(x_prompt)
    in_maps = []
    for c in range(N_CORES):
        j0 = c * NSEQ
        m = dict(common)
        m['xp'] = xp_all[c % B]
        m['xs'] = np.ascontiguousarray(xs_all[j0:j0 + NSEQ].reshape(NSEQ * 4, D))
        m['sh'] = np.ascontiguousarray(sh_all[:, j0:j0 + NSEQ])
        m['sc'] = np.ascontiguousarray(sc_all[:, j0:j0 + NSEQ].reshape(NL, NSEQ * 2, 1024))
        in_maps.append(m)
    res = run_bass_kernel_spmd(nc, in_maps, core_ids=list(range(N_CORES)))
    R = res.results
    y_prompt = np.stack([R[b]['yp'] for b in range(B)]).astype(np.float32)
    y_sample = np.concatenate([R[c]['ys'].reshape(NSEQ, 4, D) for c in range(N_CORES)], axis=0).astype(np.float32)
    nhp = np.stack([R[b]['nshp'] for b in range(B)], axis=1).astype(np.float32)
    ncp_ = np.stack([R[b]['ncp'] for b in range(B)], axis=1).astype(np.float32)
    nhs = np.concatenate([R[c]['nshs'] for c in range(N_CORES)], axis=1).astype(np.float32)
    ncs_ = np.concatenate([R[c]['ncs'].reshape(NL, NSEQ, 2, 1024) for c in range(N_CORES)], axis=1).astype(np.float32)
    return (y_prompt, y_sample, nhp, ncp_, nhs, ncs_)
```

```python
from contextlib import ExitStack

import numpy as np
import concourse.bass as bass
import concourse.mybir as mybir
from concourse.bass_utils import run_bass_kernel_spmd

F32 = mybir.dt.float32
BF16 = mybir.dt.bfloat16
AF = mybir.ActivationFunctionType
ALU = mybir.AluOpType

D = 2048
DC = 16
NH = 8
EPS = 1e-6
N_CORES = 8


class Prog:
    ENG = ['pe', 'act', 'dve', 'pool', 'sp']

    def __init__(s, nc):
        s.nc = nc
        s.ops = []
        s.tok_w = {}
        s.tok_r = {}
        s.epoch = 0
        s.dma_cnt = {}
        s.bar = None
        s.bar_kw = None

    def barrier(s):
        last = {}
        for j, o in enumerate(s.ops):
            if o['dsem'] is not None:
                last[('dma', o['dsem'])] = j
            else:
                last[o['eng']] = j
        o = dict(eng='dve', meth='memset', kw=s.bar_kw, deps=set(last.values()), dsem=None, epoch=s.epoch)
        s.ops.append(o)
        s.bar = len(s.ops) - 1

    def op(s, eng, meth, kw, r=(), w=(), dsem=None):
        i = len(s.ops)
        deps = set()
        for t in r:
            if t in s.tok_w:
                deps.add(s.tok_w[t])
        for t in w:
            if t in s.tok_w:
                deps.add(s.tok_w[t])
            for x in s.tok_r.get(t, ()):
                deps.add(x)
        if s.bar is not None:
            deps.add(s.bar)
        o = dict(eng=eng, meth=meth, kw=kw, deps=deps, dsem=dsem, epoch=s.epoch)
        if dsem is not None:
            s.dma_cnt[dsem] = s.dma_cnt.get(dsem, 0) + 16
            o['dval'] = s.dma_cnt[dsem]
        s.ops.append(o)
        for t in r:
            s.tok_r.setdefault(t, []).append(i)
        for t in w:
            s.tok_w[t] = i
            s.tok_r[t] = []
        return i

    def emit(s):
        nc = s.nc
        needed = set()
        for o in s.ops:
            for d in o['deps']:
                p = s.ops[d]
                if p['dsem'] is None:
                    if p['eng'] == 'pe' and o['eng'] == 'pe':
                        continue
                    needed.add(d)
        cnt = {}
        for i, o in enumerate(s.ops):
            if o['dsem'] is None and i in needed:
                k = (o['eng'], o['epoch'])
                cnt[k] = cnt.get(k, 0) + 1
                o['seq'] = cnt[k]
        with ExitStack() as st:
            sems = {}
            for k in cnt:
                sems[k] = st.enter_context(nc.semaphore("s_%s_%d" % k))
            for k in s.dma_cnt:
                sems[('dma', k)] = st.enter_context(nc.semaphore("d_%s" % str(k)))
            block = st.enter_context(nc.Block())
            engobj = {'pe': block.tensor, 'act': block.scalar, 'dve': block.vector,
                      'pool': block.gpsimd, 'sp': block.sync}

            def body(ename):
                def f(e):
                    waited = {}
                    for o in s.ops:
                        if o['eng'] != ename:
                            continue
                        ws = {}
                        for d in o['deps']:
                            p = s.ops[d]
                            if p['dsem'] is not None:
                                key = ('dma', p['dsem'])
                                val = p['dval']
                            else:
                                if p['eng'] == 'pe' and ename == 'pe':
                                    continue
                                key = (p['eng'], p['epoch'])
                                val = p['seq']
                            if ws.get(key, 0) < val:
                                ws[key] = val
                        for key, val in ws.items():
                            if waited.get(key, 0) >= val:
                                continue
                            e.wait_ge(sems[key], val)
                            waited[key] = val
                        ins = getattr(e, o['meth'])(**o['kw'])
                        if o['dsem'] is not None:
                            ins.then_inc(sems[('dma', o['dsem'])], 16)
                        elif 'seq' in o:
                            ins.then_inc(sems[(o['eng'], o['epoch'])], 1)
                    if ename == 'sp':
                        for k, v in s.dma_cnt.items():
                            e.wait_ge(sems[('dma', k)], v)
                return f
            for ename in s.ENG:
                engobj[ename](body(ename))


def split_pieces(T):
    n = (T + 511) // 512
    base = ((T + n - 1) // n + 15) // 16 * 16
    out = []
    c = 0
    while c < T:
        out.append((c, min(T, c + base)))
        c += base
    return out


def build_nc(NL=4, NPCH=16, NSEQ=16, NW=3, CHAIN_RATIO=3, CH=128, CHAIN_DELAY=4):
    assert NSEQ % 4 == 0 and 4 * NSEQ <= 64
    NPT = 2 * NPCH * 64
    LS = 4 * NSEQ
    T0 = 16 + 64 * NPCH
    T1 = 64 * NPCH + LS
    TM = max(T0, T1)
    NPC = NPCH * 64 // CH
    NCHM = NPC + 1
    NCOL = NCHM + NSEQ

    nc = bass.Bass("TRN2", target_bir_lowering=False)
    dt = lambda n, s, k: nc.dram_tensor(n, s, F32, kind=k).ap()
    xp = dt("xp", [NPT, D], "ExternalInput")
    meta = dt("meta", [16, D], "ExternalInput")
    xs = dt("xs", [LS, D], "ExternalInput")
    sh = dt("sh", [NL, NSEQ, NH, 128, 128], "ExternalInput")
    sc = dt("sc", [NL, NSEQ * 2, 1024], "ExternalInput")
    w_in = dt("w_in", [NL, D, 8192], "ExternalInput")
    w_out = dt("w_out", [NL, D, D], "ExternalInput")
    conv_w = dt("conv_w", [NL * 24, 128], "ExternalInput")
    lb_param = dt("lb_param", [NL * 8, 128], "ExternalInput")
    hnw = dt("hnw", [NL, 128], "ExternalInput")
    prew = dt("prew", [NL * 16, 128], "ExternalInput")
    postw = dt("postw", [NL * 16, 128], "ExternalInput")
    yp = dt("yp", [NPT, D], "ExternalOutput")
    ys = dt("ys", [LS, D], "ExternalOutput")
    nshp = dt("nshp", [NL, NH, 128, 128], "ExternalOutput")
    ncp = dt("ncp", [NL, 2, 1024], "ExternalOutput")
    nshs = dt("nshs", [NL, NSEQ, NH, 128, 128], "ExternalOutput")
    ncs = dt("ncs", [NL, NSEQ * 2, 1024], "ExternalOutput")
    sbnd = dt("sbnd", [NL, NH, 128, 128], "Internal")
    wscr = nc.dram_tensor("wscr", [NL, 80, 128, DC * 128], BF16, kind="Internal").ap()

    st = ExitStack()
    sbt = lambda n, s, d: st.enter_context(nc.sbuf_tensor(n, s, d))
    hT = sbt("hT", [128, DC, TM], F32)
    mx = sbt("mx", [128, DC, TM], BF16)
    O_HN = 0
    W_HN = DC * TM // 2
    O_T = [W_HN + i * TM for i in range(3)]
    O_G = W_HN + 3 * TM
    O_B = O_G + TM + 2
    WB = TM // 2
    O_VT = O_B + 5 * WB
    W_VT = (NCHM + 1) * 64
    PMAX = max(c1 - c0 for T_ in (T0, T1) for (c0, c1) in split_pieces(T_))
    PSQ_ALIAS = (O_B + 3 * WB >= W_HN + DC * PMAX)
    AW = max(O_VT + 2 * W_VT, W_HN + DC * PMAX) + 2 * 256 + (0 if PSQ_ALIAS else 2 * 256)
    arena = sbt("arena", [128, AW], F32)
    hn = arena[:, O_HN:O_HN + W_HN].bitcast(BF16).rearrange("p (c t) -> p c t", t=TM)
    sqw = [arena[:, AW - 512 + i * 256: AW - 512 + (i + 1) * 256].bitcast(BF16) for i in range(2)]
    psq = None if PSQ_ALIAS else [arena[:, AW - 1024 + i * 256: AW - 1024 + (i + 1) * 256].bitcast(BF16) for i in range(2)]
    t1, t2, t3 = [arena[:, o:o + TM] for o in O_T]
    if TM >= 1024:
        g1, g2, g3 = t1, t2, t3
    else:
        g1, g2, g3 = [sbt("stg%d" % i, [128, 1024], F32)[:] for i in range(3)]
    Gb = arena[:, O_G:O_G + TM + 2]
    bfb = [arena[:, O_B + i * WB:O_B + (i + 1) * WB].bitcast(BF16) for i in range(5)]
    qt, kt, kh, gate, vT = bfb
    vtok = arena[:, O_VT:O_VT + W_VT].bitcast(BF16).rearrange("p (c v) -> p c v", v=128)
    khtok = arena[:, O_VT + W_VT:O_VT + 2 * W_VT].bitcast(BF16).rearrange("p (c v) -> p c v", v=128)
    wsl = [sbt("wsl%d" % i, [128, DC, 128], BF16) for i in range(NW)]
    smask = sbt("smask", [128, TM], BF16)
    oTb = sbt("oTb", [128, TM], F32)[:]
    onec = sbt("onec", [128, 1], F32)
    Sb = [sbt("S%d" % i, [128, 128], F32) for i in range(2)]
    Sbf = [sbt("Sbf%d" % i, [128, 128], BF16) for i in range(3)]
    Am = [sbt("Am%d" % i, [CH, CH], BF16) for i in range(2)]
    AmS = sbt("AmS", [64, 64], BF16)
    NSIN = 3
    Sin = [sbt("Sin%d" % i, [128, 4, 128], F32) for i in range(NSIN)]
    SinBf = sbt("SinBf", [128, 4, 128], BF16)
    Vblk = [sbt("Vblk%d" % i, [64, 4, 128], BF16) for i in range(2)]
    seqmask = sbt("seqmask", [64, 16], F32)
    gref = sbt("gref", [128, NCOL], F32)
    eref = sbt("eref", [128, NCOL], F32)
    bcol = sbt("bcol", [128, NCOL], F32)
    acol = sbt("acol", [128, NCOL], F32)
    us = sbt("us", [128, NSEQ, 6], F32)
    scT = sbt("scT", [128, 8, NSEQ * 2], F32)
    uprev = sbt("uprev", [128, NL, 8, 2], F32)
    ncst = sbt("ncst", [128, 8, 2], F32)
    ncsS = sbt("ncsS", [128, 8, NSEQ, 2], F32)
    identf = sbt("identf", [128, 128], F32)
    identb = sbt("identb", [128, 128], BF16)
    onesb = sbt("onesb", [128, 128], BF16)
    maskc = sbt("maskc", [CH, CH], F32)
    masks = sbt("masks", [64, 64], F32)
    preT = sbt("preT", [128, NL * 16], F32)
    postT = sbt("postT", [128, NL * 16], F32)
    lbp = sbt("lbp", [128, NL, 8], F32)
    lbT = sbt("lbT", [128, NL, 8], F32)
    omlb = sbt("omlb", [128, NL, 8], F32)
    nomlb = sbt("nomlb", [128, NL, 8], F32)
    lbtmp = sbt("lbtmp", [128, 3, 8], F32)
    nwT = sbt("nwT", [128, NL], F32)
    cwT = sbt("cwT", [128, NL * 24], F32)
    dummy = sbt("dummyt", [128, 2], F32)
    psb = [st.enter_context(nc.psum_tensor("ps%d" % i, [128, 512], F32)) for i in range(8)]
    RING = [0, 1, 2, 3, 7]
    RA, RP, RO0, RO1 = 4, 5, 6, 6
    ring_i = [0]

    def ring():
        b = RING[ring_i[0] % len(RING)]
        ring_i[0] += 1
        return b

    def PB(b):
        return ('ps', b)

    P = Prog(nc)
    P.bar_kw = dict(ap=dummy[:, 0:1], constant=0.0)
    op = P.op

    op('pool', 'memset', dict(ap=identf[:], constant=1.0), w=['identf'])
    op('pool', 'affine_select', dict(out=identf[:], in_=identf[:], pattern=[[-1, 128]], compare_op=ALU.is_equal, fill=0.0, base=0, channel_multiplier=1),
       w=['identf'])
    op('dve', 'tensor_copy', dict(out=identb[:], in_=identf[:]), r=['identf'], w=['identb'])
    op('dve', 'memset', dict(ap=onesb[:], constant=1.0), w=['onesb'])
    for i_ in range(2):
        op('dve', 'memset', dict(ap=Am[i_][:], constant=0.0), w=[('Am', i_)])
    op('dve', 'memset', dict(ap=onec[:], constant=1.0), w=['onec'])
    op('dve', 'memset', dict(ap=dummy[:], constant=0.0), w=['dummy'])
    op('dve', 'memset', dict(ap=psb[RA][:, :], constant=0.0), w=[PB(RA)])
    op('pool', 'memset', dict(ap=maskc[:], constant=1.0), w=['maskc'])
    op('pool', 'affine_select', dict(out=maskc[:], in_=maskc[:], pattern=[[1, CH]], compare_op=ALU.is_ge, fill=0.0, base=0, channel_multiplier=-1),
       w=['maskc'])
    op('pool', 'memset', dict(ap=seqmask[:], constant=1.0), w=['seqmask'])
    op('pool', 'affine_select', dict(out=seqmask[:], in_=seqmask[:], pattern=[[-4, 16]], compare_op=ALU.is_ge, fill=0.0, base=0, channel_multiplier=1), w=['seqmask'])
    op('pool', 'affine_select', dict(out=seqmask[:], in_=seqmask[:], pattern=[[4, 16]], compare_op=ALU.is_ge, fill=0.0, base=3, channel_multiplier=-1), w=['seqmask'])
    op('pool', 'memset', dict(ap=masks[:], constant=1.0), w=['masks'])
    op('pool', 'affine_select', dict(out=masks[:], in_=masks[:], pattern=[[1, 64]], compare_op=ALU.is_ge, fill=0.0, base=0, channel_multiplier=-1),
       w=['masks'])
    m3 = masks[:].rearrange("p (g t) -> p g t", t=4)
    op('pool', 'affine_select', dict(out=m3, in_=m3, pattern=[[-4, 16], [0, 4]], compare_op=ALU.is_ge, fill=0.0, base=0, channel_multiplier=1),
       w=['masks'])
    op('pool', 'affine_select', dict(out=m3, in_=m3, pattern=[[4, 16], [0, 4]], compare_op=ALU.is_ge, fill=0.0, base=3, channel_multiplier=-1),
       w=['masks'])

    def load_T(src, nrows, dst_ap, key):
        stg = g1[0:nrows, 0:128]
        op('sp', 'dma_start', dict(out=stg, in_=src), w=['t1'], dsem='io0')
        b = ring()
        op('pe', 'transpose', dict(out=psb[b][:, 0:nrows], in_=stg, identity=identf[0:nrows, 0:nrows]),
           r=['t1', 'identf'], w=[PB(b)])
        op('dve', 'tensor_copy', dict(out=dst_ap, in_=psb[b][:, 0:nrows]), w=[PB(b), key])

    load_T(prew[:, :], NL * 16, preT[:], 'preT')
    load_T(postw[:, :], NL * 16, postT[:], 'postT')
    load_T(lb_param[:, :], NL * 8, lbp[:].rearrange("p l h -> p (l h)"), 'lbp')
    load_T(hnw[:, :], NL, nwT[:], 'nwT')
    load_T(conv_w[:, :], NL * 24, cwT[:], 'cwT')
    mxl, ssum, rs = lbtmp[:, 0, :], lbtmp[:, 1, :], lbtmp[:, 2, :]
    op('dve', 'tensor_copy', dict(out=mxl, in_=lbp[:, 0, :]), r=['lbp'], w=['lbtmp'])
    for l in range(1, NL):
        op('dve', 'tensor_tensor', dict(out=mxl, in0=mxl, in1=lbp[:, l, :], op=ALU.max),
           r=['lbp'], w=['lbtmp'])
    for l in range(NL):
        op('dve', 'tensor_tensor', dict(out=lbp[:, l, :], in0=lbp[:, l, :], in1=mxl, op=ALU.subtract),
           r=['lbtmp'], w=['lbp'])
    op('act', 'activation', dict(out=lbp[:], in_=lbp[:], func=AF.Exp), w=['lbp'])
    op('dve', 'tensor_copy', dict(out=ssum, in_=lbp[:, 0, :]), r=['lbp'], w=['lbtmp'])
    for l in range(1, NL):
        op('dve', 'tensor_tensor', dict(out=ssum, in0=ssum, in1=lbp[:, l, :], op=ALU.add),
           r=['lbp'], w=['lbtmp'])
    op('dve', 'reciprocal', dict(out=rs, in_=ssum), w=['lbtmp'])
    op('dve', 'memset', dict(ap=lbT[:, 0, :], constant=0.0), w=['lbT'])
    for l in range(1, NL):
        op('dve', 'tensor_tensor', dict(out=lbp[:, l, :], in0=lbp[:, l, :], in1=rs, op=ALU.mult),
           r=['lbtmp'], w=['lbp'])
        op('dve', 'tensor_tensor', dict(out=lbT[:, l, :], in0=lbT[:, l - 1, :], in1=lbp[:, l, :], op=ALU.add),
           r=['lbp'], w=['lbT'])
    op('dve', 'tensor_scalar', dict(out=omlb[:], in0=lbT[:], scalar1=-1.0, scalar2=1.0, op0=ALU.mult, op1=ALU.add),
       r=['lbT'], w=['omlb'])
    op('dve', 'tensor_scalar', dict(out=nomlb[:], in0=lbT[:], scalar1=1.0, scalar2=-1.0, op0=ALU.mult, op1=ALU.add),
       r=['lbT'], w=['nomlb'])
    CONSTS = ['identf', 'identb', 'onesb', 'maskc', 'masks', 'preT', 'postT', 'lbT', 'omlb', 'nomlb', 'nwT', 'cwT']

    wstate = dict(n=0)

    wseen = set()

    def wload(item):
        l_, eid, src3 = item
        k = wstate['n'] % NW
        wstate['n'] += 1
        qt_ = [('w', k, c0) for c0 in range(0, DC, 4)]
        if (l_, eid) not in wseen:
            wseen.add((l_, eid))
            for c0 in range(0, DC, 4):
                op('pool', 'dma_start', dict(out=wsl[k][:, c0:c0 + 4, :], in_=src3[:, c0:c0 + 4, :]),
                   w=[('w', k, c0)], dsem='w%d_%d' % (k, c0))
            op('sp', 'dma_start', dict(out=wscr[l_, eid], in_=wsl[k][:].rearrange("p c e -> p (c e)")),
               r=qt_, w=[('wscr', l_, eid)], dsem='wst%d' % k)
        else:
            op('sp', 'dma_start', dict(out=wsl[k][:].rearrange("p c e -> p (c e)"), in_=wscr[l_, eid]),
               r=[('wscr', l_, eid)], w=qt_, dsem='wld%d' % k)
        return k

    def win_src(l, col0):
        return w_in[l].rearrange("(c p) e -> p c e", p=128)[:, :, col0:col0 + 128]

    def wout_src(l, dd):
        return w_out[l].rearrange("(c p) e -> p c e", p=128)[:, :, dd * 128:(dd + 1) * 128]

    for sb in range(2):
        if sb == 0:
            T = T0
            chunks = [(0, 16)] + [(16 + CH * i, CH) for i in range(NPC)]
            samp = None
            Tp = T0
            iot = [(meta[:, :], None, 16, 0)] + \
                  [(xp[i * 128:(i + 1) * 128, :], yp[i * 128:(i + 1) * 128, :], 128, 16 + 128 * i)
                   for i in range(NPCH // 2)]
        else:
            T = T1
            chunks = [(CH * i, CH) for i in range(NPC)]
            samp = 64 * NPCH
            Tp = 64 * NPCH
            hb = NPCH * 64
            iot = [(xp[hb + i * 128:hb + (i + 1) * 128, :], yp[hb + i * 128:hb + (i + 1) * 128, :], 128, 128 * i)
                   for i in range(NPCH // 2)] + [(xs[:, :], ys[:, :], LS, samp)]
        nch = len(chunks)
        pieces = split_pieces(T)
        pch_lo = 1 if sb == 0 else 0
        pc0 = chunks[pch_lo][0]
        npc = nch - pch_lo

        def v64(ap, lo=pc0, n=npc):
            return ap[:, lo:lo + CH * n].rearrange("p (c l) -> p c l", l=CH)

        P.epoch += 1
        op('dve', 'memset', dict(ap=smask[:], constant=1.0), w=['smask'])
        if sb == 0:
            op('dve', 'memset', dict(ap=smask[:, 0:1], constant=0.0), w=['smask'])
        op('dve', 'memset', dict(ap=v64(smask)[:, :, 0:1], constant=0.0), w=['smask'])
        if samp is not None:
            op('dve', 'memset', dict(ap=smask[:, samp:samp + LS].rearrange("p (j t) -> p j t", t=4)[:, :, 0:1], constant=0.0),
               w=['smask'])
        P.barrier()
        for ti, (src, _dst, ntok, col0) in enumerate(iot):
            for half in range(2):
                stg_full = (g2, g3)[half]
                key = ('t2', 't3')[half]
                stg = stg_full[0:ntok, 0:1024]
                op('sp', 'dma_start', dict(out=stg, in_=src[:, half * 1024:(half + 1) * 1024]),
                   r=['ARENA'], w=[key], dsem='io%d' % (1 + half))
                for q in range(2):
                    b = ring()
                    for j in range(4):
                        dcl = q * 4 + j
                        op('pe', 'transpose', dict(out=psb[b][:, j * 128:j * 128 + ntok], in_=stg[:, dcl * 128:(dcl + 1) * 128], identity=identf[0:ntok, 0:ntok]), r=[key, 'identf'], w=[PB(b)])
                    d0 = half * 8 + q * 4
                    eng = 'act' if q == 0 else 'dve'
                    srcp = psb[b][:, :].rearrange("p (j t) -> p j t", t=128)[:, :, 0:ntok]
                    dstp = hT[:, d0:d0 + 4, col0:col0 + ntok]
                    if eng == 'act':
                        op('act', 'activation', dict(out=dstp, in_=srcp, func=AF.Copy),
                           w=[PB(b)] + [('hT', d0 + j) for j in range(4)])
                    else:
                        op('dve', 'tensor_copy', dict(out=dstp, in_=srcp),
                           w=[PB(b)] + [('hT', d0 + j) for j in range(4)])

        for l in range(NL):
            P.epoch += 1
            def prenorm_gen(lp, pi):
                c0, c1 = pieces[pi]
                N = c1 - c0
                b = RO0
                for d in range(DC):
                    if PSQ_ALIAS:
                        sqb = bfb[3 + d % 2][:, 0:N]
                        sk = ('bfb', 3 + d % 2)
                    else:
                        sqb = psq[d % 2][:, 0:N]
                        sk = ('psq', d % 2)
                    op('act', 'activation', dict(out=sqb, in_=hT[:, d, c0:c1], func=AF.Square),
                       r=[('hT', d)], w=[sk])
                    yield
                    op('pe', 'matmul', dict(out=psb[b][:, 0:N], lhsT=onesb[:], rhs=sqb, start=(d == 0), stop=(d == DC - 1)),
                       r=[sk, 'onesb'], w=[PB(b)])
                op('dve', 'tensor_scalar', dict(out=oTb[:, c0:c1], in0=psb[b][:, 0:N], scalar1=1.0 / D, scalar2=EPS, op0=ALU.mult, op1=ALU.add),
                   w=[PB(b), 'oT'])
                op('act', 'activation', dict(out=oTb[:, c0:c1], in_=oTb[:, c0:c1], func=AF.Ln), w=['oT'])
                op('act', 'activation', dict(out=oTb[:, c0:c1], in_=oTb[:, c0:c1], func=AF.Exp, scale=-0.5), w=['oT'])
                for d in range(DC):
                    op('dve', 'scalar_tensor_tensor', dict(out=hn[:, d, c0:c1], in0=hT[:, d, c0:c1], scalar=preT[:, lp * 16 + d:lp * 16 + d + 1], in1=oTb[:, c0:c1], op0=ALU.mult, op1=ALU.mult),
                       r=[('hT', d), 'oT', 'preT'], w=[('hn', d, pi)])

            if l == 0:
                P.barrier()
                for pi in range(len(pieces)):
                    for _ in prenorm_gen(0, pi):
                        pass

            def proj(slot):
                res = []
                for (c0, c1) in pieces:
                    b = ring()
                    N = c1 - c0
                    for d in range(DC):
                        op('pe', 'matmul', dict(out=psb[b][:, 0:N], lhsT=wsl[slot][:, d, :], rhs=hn[:, d, c0:c1], start=(d == 0), stop=(d == DC - 1)),
                           r=[('w', slot, (d // 4) * 4), ('hn', d)], w=[PB(b)])
                    res.append((b, c0, c1))
                return res

            wq = []
            for h in range(NH):
                for qi in (1, 2, 3, 0):
                    wq.append((l, qi * 8 + h, win_src(l, qi * 1024 + h * 128)))
                for qi in (5, 6, 4, 7):
                    wq.append((l, qi * 8 + h, win_src(l, qi * 1024 + h * 128)))
            for pi_ in range(len(pieces)):
                for dd in range(DC):
                    wq.append((l, 64 + dd, wout_src(l, dd)))
            wpos = dict(i=0, slots=[])

            def wnext():
                while len(wpos['slots']) < NW - 1 + 1 and wpos['i'] < len(wq):
                    wpos['slots'].append(wload(wq[wpos['i']]))
                    wpos['i'] += 1
                return wpos['slots'].pop(0)

            if samp is not None:
                stg = g1[0:NSEQ * 2, 0:1024]
                op('sp', 'dma_start', dict(out=stg, in_=sc[l]), r=['ARENA'], w=['t1'], dsem='io0')
                b = ring()
                for cb in range(8):
                    op('pe', 'transpose', dict(out=psb[b][:, cb * NSEQ * 2:(cb + 1) * NSEQ * 2], in_=stg[:, cb * 128:(cb + 1) * 128], identity=identf[0:NSEQ * 2, 0:NSEQ * 2]), r=['t1', 'identf'], w=[PB(b)])
                op('dve', 'tensor_copy', dict(out=scT[:].rearrange("p c n -> p (c n)"), in_=psb[b][:, 0:8 * NSEQ * 2]),
                   w=[PB(b), 'scT'])

            def proj_gen(slot):
                for pi_, (c0, c1) in enumerate(pieces):
                    b = ring()
                    N = c1 - c0
                    for d in range(DC):
                        op('pe', 'matmul', dict(out=psb[b][:, 0:N], lhsT=wsl[slot][:, d, :], rhs=hn[:, d, c0:c1], start=(d == 0), stop=(d == DC - 1)),
                           r=[('w', slot, (d // 4) * 4), ('hn', d, pi_)], w=[PB(b)])
                    yield (b, c0, c1)

            tchunks = list(chunks) + ([(samp, LS)] if samp is not None else [])

            def tok_major(srcb, srck, dst, dstk, eng):
                for g0 in range(0, len(tchunks), 8):
                    grp = tchunks[g0:g0 + 8]
                    b = ring()
                    pbv = psb[b][:, :].bitcast(BF16)
                    for j, (cc0, L) in enumerate(grp):
                        op('pe', 'transpose', dict(out=pbv[0:L, j * 128:(j + 1) * 128], in_=srcb[:, cc0:cc0 + L], identity=identb[:]),
                           r=[srck, 'identb'], w=[PB(b)])
                    j = 0
                    while j < len(grp):
                        L = grp[j][1]
                        j1 = j
                        while j1 < len(grp) and grp[j1][1] == L:
                            j1 += 1
                        srcp = pbv[0:L, j * 128:j1 * 128].rearrange("p (c v) -> p c v", v=128)
                        dstp = dst[0:L, g0 + j:g0 + j1, :]
                        if eng == 'act':
                            op('act', 'activation', dict(out=dstp, in_=srcp, func=AF.Copy), w=[PB(b), dstk])
                        else:
                            op('dve', 'tensor_copy', dict(out=dstp, in_=srcp), w=[PB(b), dstk])
                        j = j1

            def front(h, side=None):
                lbc = lbT[:, l, h:h + 1]
                omc = omlb[:, l, h:h + 1]
                nomc = nomlb[:, l, h:h + 1]
                sl = lambda ap: ap[:, 0:T]
                slot = wnext()
                for (b, c0, c1) in proj_gen(slot):
                    op('act', 'activation', dict(out=t1[:, c0:c1], in_=psb[b][:, 0:c1 - c0], func=AF.Exp, scale=-1.0),
                       w=[PB(b), 't1'])
                    if side is not None:
                        next(side, None)
                if side is not None:
                    for _ in side:
                        pass
                op('dve', 'tensor_scalar', dict(out=sl(t1), in0=sl(t1), scalar1=1e30, scalar2=1.0, op0=ALU.min, op1=ALU.add), w=['t1'])
                op('act', 'activation', dict(out=sl(t1), in_=sl(t1), func=AF.Ln), w=['t1'])
                op('act', 'activation', dict(out=sl(t1), in_=sl(t1), func=AF.Exp, scale=-1.0), w=['t1'])
                slot = wnext()
                for (b, c0, c1) in proj_gen(slot):
                    op('act', 'activation', dict(out=vT[:, c0:c1], in_=psb[b][:, 0:c1 - c0], func=AF.Copy),
                       w=[PB(b), ('bfb', 4)])
                op('dve', 'tensor_scalar', dict(out=sl(t2), in0=sl(t1), scalar1=omc, scalar2=lbc, op0=ALU.mult, op1=ALU.add),
                   r=['t1', 'omlb', 'lbT'], w=['t2'])
                op('act', 'activation', dict(out=sl(t2), in_=sl(t2), func=AF.Ln), w=['t2'])
                op('dve', 'tensor_scalar', dict(out=sl(t1), in0=sl(t1), scalar1=nomc, scalar2=omc, op0=ALU.mult, op1=ALU.add),
                   r=['nomlb', 'omlb'], w=['t1'])
                slot = wnext()
                for (b, c0, c1) in proj_gen(slot):
                    N = c1 - c0
                    op('act', 'activation', dict(out=oTb[:, c0:c1], in_=psb[b][:, 0:N], func=AF.Exp, scale=-1.0),
                       w=[PB(b), 'oT'])
                    op('act', 'activation', dict(out=oTb[:, c0:c1], in_=oTb[:, c0:c1], func=AF.Ln, bias=onec[:, 0:1]), r=['onec'], w=['oT'])
                    op('act', 'activation', dict(out=oTb[:, c0:c1], in_=oTb[:, c0:c1], func=AF.Exp, scale=-1.0), w=['oT'])
                    op('dve', 'tensor_tensor', dict(out=gate[:, c0:c1], in0=psb[b][:, 0:N], in1=oTb[:, c0:c1], op=ALU.mult),
                       r=['oT'], w=[PB(b), ('bfb', 3)])
                Gv = Gb[:, 0:T]
                op('dve', 'tensor_tensor_scan', dict(out=Gv, data0=smask[:, 0:T], data1=sl(t2), initial=0.0, op0=ALU.mult, op1=ALU.add),
                   r=['t2', 'smask'], w=['G'])
                if sb == 0:
                    op('dve', 'tensor_copy', dict(out=gref[:, 0:1], in_=Gb[:, 7:8]), r=['G'], w=['gref'])
                op('dve', 'tensor_copy', dict(out=gref[:, pch_lo:nch], in_=v64(Gb)[:, :, CH // 2 - 1]), r=['G'], w=['gref'])
                if sb == 0:
                    op('dve', 'tensor_scalar', dict(out=Gb[:, 0:16], in0=Gb[:, 0:16], scalar1=gref[:, 0:1], scalar2=None, op0=ALU.subtract),
                       r=['gref'], w=['G'])
                op('dve', 'tensor_tensor', dict(out=v64(Gb), in0=v64(Gb), in1=gref[:, pch_lo:nch].unsqueeze(2).to_broadcast([128, npc, CH]), op=ALU.subtract),
                   r=['gref'], w=['G'])
                op('act', 'activation', dict(out=sl(t2), in_=Gv, func=AF.Exp), r=['G'], w=['t2'])
                op('act', 'activation', dict(out=sl(t3), in_=Gv, func=AF.Exp, scale=-1.0), r=['G'], w=['t3'])
                if sb == 0:
                    op('dve', 'tensor_tensor', dict(out=bcol[:, 0:1], in0=Gb[:, 15:16], in1=gref[:, 0:1], op=ALU.add), r=['G', 'gref'], w=['bcol'])
                op('dve', 'tensor_tensor', dict(out=bcol[:, pch_lo:nch], in0=v64(Gb)[:, :, CH - 1], in1=gref[:, pch_lo:nch], op=ALU.add),
                   r=['G', 'gref'], w=['bcol'])
                ncol_used = nch
                if samp is not None:
                    op('dve', 'tensor_copy', dict(out=bcol[:, nch:nch + NSEQ], in_=Gb[:, samp:samp + LS].rearrange("p (j t) -> p j t", t=4)[:, :, 3]),
                       r=['G'], w=['bcol'])
                    ncol_used = nch + NSEQ
                op('act', 'activation', dict(out=bcol[:, 0:ncol_used], in_=bcol[:, 0:ncol_used], func=AF.Exp), w=['bcol'])
                op('act', 'activation', dict(out=eref[:, 0:nch], in_=gref[:, 0:nch], func=AF.Exp), r=['gref'], w=['eref'])
                if sb == 0:
                    op('dve', 'tensor_copy', dict(out=acol[:, 0:1], in_=t2[:, 15:16]), r=['t2'], w=['acol'])
                op('dve', 'tensor_copy', dict(out=acol[:, pch_lo:nch], in_=v64(t2)[:, :, CH - 1]), r=['t2'], w=['acol'])
                if samp is not None:
                    op('dve', 'tensor_copy', dict(out=acol[:, nch:nch + NSEQ], in_=t2[:, samp:samp + LS].rearrange("p (j t) -> p j t", t=4)[:, :, 3]),
                       r=['t2'], w=['acol'])
                tok_major(vT, ('bfb', 4), vtok, 'vtok', 'act')
                slot = wnext()
                for (b, c0, c1) in proj_gen(slot):
                    op('dve', 'tensor_tensor', dict(out=qt[:, c0:c1], in0=psb[b][:, 0:c1 - c0], in1=t2[:, c0:c1], op=ALU.mult),
                       r=['t2'], w=[PB(b), ('bfb', 0)])
                op('dve', 'tensor_tensor', dict(out=sl(kt), in0=sl(t1), in1=sl(t3), op=ALU.mult), r=['t1', 't3'], w=[('bfb', 1)])
                if sb == 0:
                    op('dve', 'tensor_scalar', dict(out=kh[:, 0:16], in0=kt[:, 0:16], scalar1=acol[:, 0:1], scalar2=None, op0=ALU.mult),
                       r=[('bfb', 1), 'acol'], w=[('bfb', 2)])
                op('dve', 'tensor_tensor', dict(out=v64(kh), in0=v64(kt), in1=acol[:, pch_lo:nch].unsqueeze(2).to_broadcast([128, npc, CH]), op=ALU.mult),
                   r=[('bfb', 1), 'acol'], w=[('bfb', 2)])
                if samp is not None:
                    s4 = lambda ap: ap[:, samp:samp + LS].rearrange("p (j t) -> p j t", t=4)
                    op('dve', 'tensor_tensor', dict(out=s4(kh), in0=s4(kt), in1=acol[:, nch:nch + NSEQ].unsqueeze(2).to_broadcast([128, NSEQ, 4]), op=ALU.mult),
                       r=[('bfb', 1), 'acol'], w=[('bfb', 2)])

            def sin_load(h, qd):
                j0 = qd * 4
                si = (h * (NSEQ // 4) + qd) % NSIN
                op('sp', 'dma_start', dict(out=Sin[si][:], in_=sh[l, j0:j0 + 4, h].rearrange("j k v -> k j v")),
                   w=[('Sin', si)], dsem='Sin%d' % si)

            def vblk_build(qd):
                j0 = qd * 4
                op('pool', 'tensor_tensor', dict(out=Vblk[qd % 2][0:LS], in0=vtok[0:LS, nch, :].unsqueeze(1).to_broadcast([LS, 4, 128]),
                                                in1=seqmask[0:LS, j0:j0 + 4].unsqueeze(2).to_broadcast([LS, 4, 128]), op=ALU.mult),
                   r=['vtok', 'seqmask'], w=[('Vblk', qd % 2)])

            def chain_gen(h):
                oT = oTb
                if samp is not None:
                    sin_load(h, 0)
                    sin_load(h, 1)
                for _ in range(CHAIN_DELAY):
                    yield
                tok_major(kh, ('bfb', 2), khtok, 'khtok', 'dve')
                if samp is not None:
                    vblk_build(0)
                    if NSEQ // 4 > 1:
                        vblk_build(1)
                if sb == 0:
                    op('dve', 'memset', dict(ap=Sb[0][:], constant=0.0), w=[('S', 0)])
                else:
                    op('sp', 'dma_start', dict(out=Sb[0][:], in_=sbnd[l, h]), r=['sbnd%d_%d' % (l, h)], w=[('S', 0)], dsem='S0')
                op('act', 'activation', dict(out=Sbf[0][:], in_=Sb[0][:], func=AF.Identity, scale=eref[:, 0:1]),
                   r=[('S', 0), 'eref'], w=[('Sbf', 0)])
                ob_i = 0
                win = None
                for ci in range(nch + 1):
                    if ci < nch:
                        cc0, L = chunks[ci]
                        a_i = ci % 2
                        n_i = (ci + 1) % 2
                        if L > 64:
                            Hh = L // 2
                            op('pe', 'matmul', dict(out=psb[RA][0:L, Hh:L], lhsT=kt[:, cc0:cc0 + L], rhs=qt[:, cc0 + Hh:cc0 + L], start=True, stop=True),
                               r=[('bfb', 1), ('bfb', 0)], w=[PB(RA)])
                            op('pe', 'matmul', dict(out=psb[RA][0:Hh, 0:Hh], lhsT=kt[:, cc0:cc0 + Hh], rhs=qt[:, cc0:cc0 + Hh], start=True, stop=True),
                               r=[('bfb', 1), ('bfb', 0)], w=[PB(RA)])
                        else:
                            op('pe', 'matmul', dict(out=psb[RA][0:L, 0:L], lhsT=kt[:, cc0:cc0 + L], rhs=qt[:, cc0:cc0 + L], start=True, stop=True),
                               r=[('bfb', 1), ('bfb', 0)], w=[PB(RA)])
                        op('pe', 'matmul', dict(out=psb[RP][:, 0:128], lhsT=khtok[0:L, ci, :], rhs=vtok[0:L, ci, :], start=True, stop=True),
                           r=['khtok', 'vtok'], w=[PB(RP)])
                        op('dve', 'copy_predicated', dict(out=Am[a_i][0:L, 0:L], mask=maskc[0:L, 0:L].bitcast(mybir.dt.uint32), data=psb[RA][0:L, 0:L]),
                           r=['maskc'], w=[PB(RA), ('Am', a_i)])
                        op('dve', 'scalar_tensor_tensor', dict(out=Sb[n_i][:], in0=Sb[ci % 2][:], scalar=bcol[:, ci:ci + 1], in1=psb[RP][:, 0:128], op0=ALU.mult, op1=ALU.add),
                           r=[('S', ci % 2), 'bcol'], w=[PB(RP), ('S', n_i)])
                        if ci + 1 < nch:
                            op('act', 'activation', dict(out=Sbf[(ci + 1) % 3][:], in_=Sb[n_i][:], func=AF.Identity, scale=eref[:, ci + 1:ci + 2]),
                               r=[('S', n_i), 'eref'], w=[('Sbf', (ci + 1) % 3)])
                    if ci >= 1:
                        cj = ci - 1
                        cc0, L = chunks[cj]
                        if win is None:
                            win = [(RO0, RO1)[ob_i % 2], cc0]
                            ob_i += 1
                        ob, w0 = win
                        oc = cc0 - w0
                        op('pe', 'matmul', dict(out=psb[ob][:, oc:oc + L], lhsT=vtok[0:L, cj, :], rhs=Am[cj % 2][0:L, 0:L], start=True, stop=False),
                           r=['vtok', ('Am', cj % 2)], w=[PB(ob)])
                        op('pe', 'matmul', dict(out=psb[ob][:, oc:oc + L], lhsT=Sbf[cj % 3][:], rhs=qt[:, cc0:cc0 + L], start=False, stop=True),
                           r=[('Sbf', cj % 3), ('bfb', 0)], w=[PB(ob)])
                        nxt_end = (chunks[cj + 1][0] + chunks[cj + 1][1] - w0) if cj + 1 < nch else None
                        if nxt_end is None or nxt_end > 512:
                            wlen = cc0 + L - w0
                            op('act', 'activation', dict(out=oT[:, w0:w0 + wlen], in_=psb[ob][:, 0:wlen], func=AF.Copy),
                               w=[PB(ob), 'oT'])
                            win = None
                    yield
                fin = nch % 2
                if sb == 0:
                    op('sp', 'dma_start', dict(out=sbnd[l, h], in_=Sb[fin][:]), r=[('S', fin)], w=['sbnd%d_%d' % (l, h)], dsem='Sst')
                else:
                    op('sp', 'dma_start', dict(out=nshp[l, h], in_=Sb[fin][:]), r=[('S', fin)], dsem='Sst')
                if samp is not None:
                    ci_s = nch
                    op('pe', 'matmul', dict(out=psb[RA][0:LS, 0:LS], lhsT=kt[:, samp:samp + LS], rhs=qt[:, samp:samp + LS], start=True, stop=True),
                       r=[('bfb', 1), ('bfb', 0)], w=[PB(RA)])
                    op('dve', 'tensor_tensor', dict(out=AmS[0:LS, 0:LS], in0=psb[RA][0:LS, 0:LS], in1=masks[0:LS, 0:LS], op=ALU.mult),
                       r=['masks'], w=[PB(RA), 'AmS'])
                    ob = (RO0, RO1)[ob_i % 2]
                    ob_i += 1
                    op('pe', 'matmul', dict(out=psb[ob][:, 0:LS], lhsT=vtok[0:LS, ci_s, :], rhs=AmS[0:LS, 0:LS], start=True, stop=False),
                       r=['vtok', 'AmS'], w=[PB(ob)])
                    yield
                    NQ = NSEQ // 4
                    for qd in range(NQ):
                        j0 = qd * 4
                        si = (h * NQ + qd) % NSIN
                        sk = ('Sin', si)
                        Sq = Sin[si]
                        if qd + 2 < NQ:
                            sin_load(h, qd + 2)
                        op('act', 'activation', dict(out=SinBf[:], in_=Sq[:], func=AF.Copy), r=[sk], w=['SinBf'])
                        Vb = Vblk[qd % 2]
                        vk = ('Vblk', qd % 2)
                        yield
                        for jq in range(4):
                            j = j0 + jq
                            last = (j == NSEQ - 1)
                            op('pe', 'matmul', dict(out=psb[ob][:, 4 * j:4 * j + 4], lhsT=SinBf[:, jq, :], rhs=qt[:, samp + 4 * j:samp + 4 * j + 4], start=False, stop=last),
                               r=['SinBf', ('bfb', 0)], w=[PB(ob)])
                        op('pe', 'matmul', dict(out=psb[RP][:, 0:512], lhsT=khtok[0:LS, ci_s, :], rhs=Vb[0:LS].rearrange("p j v -> p (j v)"), start=True, stop=True),
                           r=['khtok', vk], w=[PB(RP)])
                        if qd + 2 < NQ:
                            vblk_build(qd + 2)
                        for jq in range(4):
                            j = j0 + jq
                            op('dve', 'scalar_tensor_tensor', dict(out=Sq[:, jq, :], in0=Sq[:, jq, :], scalar=bcol[:, nch + j:nch + j + 1], in1=psb[RP][:, jq * 128:(jq + 1) * 128], op0=ALU.mult, op1=ALU.add),
                               r=['bcol'], w=[PB(RP), sk])
                        op('sp', 'dma_start', dict(out=nshs[l, j0:j0 + 4, h].rearrange("j k v -> k j v"), in_=Sq[:]),
                           r=[sk], dsem='So%d' % si)
                        yield
                    op('act', 'activation', dict(out=oT[:, samp:samp + LS], in_=psb[ob][:, 0:LS], func=AF.Copy),
                       w=[PB(ob), 'oT'])
                    yield

            def norm_gen(h):
                oT = oTb
                for (c0, c1) in pieces:
                    op('act', 'activation', dict(out=kt[:, c0:c1], in_=oT[:, c0:c1], func=AF.Square),
                       r=['oT'], w=[('bfb', 1)])
                yield
                banks = []
                for (c0, c1) in pieces:
                    N = c1 - c0
                    b = ring()
                    banks.append(b)
                    op('pe', 'matmul', dict(out=psb[b][:, 0:N], lhsT=onesb[:], rhs=kt[:, c0:c1], start=True, stop=True),
                       r=[('bfb', 1), 'onesb'], w=[PB(b)])
                yield
                for b, (c0, c1) in zip(banks, pieces):
                    N = c1 - c0
                    op('dve', 'tensor_scalar', dict(out=t3[:, c0:c1], in0=psb[b][:, 0:N], scalar1=1.0 / 128, scalar2=EPS, op0=ALU.mult, op1=ALU.add),
                       w=[PB(b), 't3'])
                    op('act', 'activation', dict(out=t3[:, c0:c1], in_=t3[:, c0:c1], func=AF.Ln), w=['t3'])
                    op('act', 'activation', dict(out=t3[:, c0:c1], in_=t3[:, c0:c1], func=AF.Exp, scale=-0.5), w=['t3'])
                    op('dve', 'scalar_tensor_tensor', dict(out=oT[:, c0:c1], in0=oT[:, c0:c1], scalar=nwT[:, l:l + 1], in1=t3[:, c0:c1], op0=ALU.mult, op1=ALU.mult),
                       r=['t3', 'nwT'], w=['oT'])
                    op('dve', 'tensor_tensor', dict(out=mx[:, h, c0:c1], in0=oT[:, c0:c1], in1=gate[:, c0:c1], op=ALU.mult),
                       r=['oT', ('bfb', 3)], w=[('mx', h)])

            def cb_gen(cb):
                w0c = cwT[:, l * 24 + 0 * 8 + cb:l * 24 + 0 * 8 + cb + 1]
                w1c = cwT[:, l * 24 + 1 * 8 + cb:l * 24 + 1 * 8 + cb + 1]
                w2c = cwT[:, l * 24 + 2 * 8 + cb:l * 24 + 2 * 8 + cb + 1]
                ub = Gb
                slot = wnext()
                for (b, c0, c1) in proj_gen(slot):
                    op('act', 'activation', dict(out=t1[:, c0:c1], in_=psb[b][:, 0:c1 - c0], func=AF.Copy),
                       w=[PB(b), 't1'])
                    yield
                slot = wnext()
                for (b, c0, c1) in proj_gen(slot):
                    op('dve', 'tensor_tensor', dict(out=ub[:, 2 + c0:2 + c1], in0=psb[b][:, 0:c1 - c0], in1=t1[:, c0:c1], op=ALU.mult),
                       r=['t1'], w=[PB(b), 'G'])
                    yield
                if sb == 0:
                    op('dve', 'memset', dict(ap=ub[:, 0:2], constant=0.0), w=['G'])
                else:
                    op('dve', 'tensor_copy', dict(out=ub[:, 0:2], in_=uprev[:, l, cb, :]), r=['uprev'], w=['G'])
                op('dve', 'tensor_scalar', dict(out=t2[:, 0:Tp], in0=ub[:, 2:2 + Tp], scalar1=w2c, scalar2=None, op0=ALU.mult),
                   r=['G', 'cwT'], w=['t2'])
                op('dve', 'scalar_tensor_tensor', dict(out=t2[:, 0:Tp], in0=ub[:, 1:1 + Tp], scalar=w1c, in1=t2[:, 0:Tp], op0=ALU.mult, op1=ALU.add),
                   r=['G', 'cwT'], w=['t2'])
                op('dve', 'scalar_tensor_tensor', dict(out=t2[:, 0:Tp], in0=ub[:, 0:Tp], scalar=w0c, in1=t2[:, 0:Tp], op0=ALU.mult, op1=ALU.add),
                   r=['G', 'cwT'], w=['t2'])
                if sb == 0:
                    op('dve', 'tensor_copy', dict(out=uprev[:, l, cb, :], in_=ub[:, Tp:Tp + 2]), r=['G'], w=['uprev'])
                else:
                    op('dve', 'tensor_copy', dict(out=ncst[:, cb, :], in_=ub[:, Tp:Tp + 2]), r=['G'], w=['ncst'])
                if samp is not None:
                    op('dve', 'tensor_copy', dict(out=us[:, :, 0:2], in_=scT[:, cb, :].rearrange("p (j r) -> p j r", r=2)),
                       r=['scT'], w=['us'])
                    op('dve', 'tensor_copy', dict(out=us[:, :, 2:6], in_=ub[:, 2 + samp:2 + samp + LS].rearrange("p (j t) -> p j t", t=4)),
                       r=['G'], w=['us'])
                    t2s = t2[:, samp:samp + LS].rearrange("p (j t) -> p j t", t=4)
                    op('dve', 'tensor_scalar', dict(out=t2s, in0=us[:, :, 2:6], scalar1=w2c, scalar2=None, op0=ALU.mult),
                       r=['us', 'cwT'], w=['t2'])
                    op('dve', 'scalar_tensor_tensor', dict(out=t2s, in0=us[:, :, 1:5], scalar=w1c, in1=t2s, op0=ALU.mult, op1=ALU.add),
                       r=['us', 'cwT'], w=['t2'])
                    op('dve', 'scalar_tensor_tensor', dict(out=t2s, in0=us[:, :, 0:4], scalar=w0c, in1=t2s, op0=ALU.mult, op1=ALU.add),
                       r=['us', 'cwT'], w=['t2'])
                    op('dve', 'tensor_copy', dict(out=ncsS[:, cb, :, :], in_=us[:, :, 4:6]), r=['us'], w=['ncsS'])
                slot = wnext()
                for (b, c0, c1) in proj_gen(slot):
                    op('dve', 'tensor_tensor', dict(out=t2[:, c0:c1], in0=psb[b][:, 0:c1 - c0], in1=t2[:, c0:c1], op=ALU.mult),
                       w=[PB(b), 't2'])
                    yield
                slot = wnext()
                for (b, c0, c1) in proj_gen(slot):
                    N = c1 - c0
                    op('act', 'activation', dict(out=t3[:, c0:c1], in_=psb[b][:, 0:N], func=AF.Exp, scale=-1.0),
                       w=[PB(b), 't3'])
                    op('act', 'activation', dict(out=t3[:, c0:c1], in_=t3[:, c0:c1], func=AF.Ln, bias=onec[:, 0:1]), r=['onec'], w=['t3'])
                    op('act', 'activation', dict(out=t3[:, c0:c1], in_=t3[:, c0:c1], func=AF.Exp, scale=-1.0), w=['t3'])
                    op('dve', 'tensor_tensor', dict(out=t3[:, c0:c1], in0=psb[b][:, 0:N], in1=t3[:, c0:c1], op=ALU.mult),
                       w=[PB(b), 't3'])
                    op('dve', 'tensor_tensor', dict(out=mx[:, 8 + cb, c0:c1], in0=t2[:, c0:c1], in1=t3[:, c0:c1], op=ALU.mult),
                       r=['t2', 't3'], w=[('mx', 8 + cb)])
                    yield

            def merge(ga, gb, ra):
                da = db = False
                while not (da and db):
                    if not db:
                        try:
                            next(gb)
                        except StopIteration:
                            db = True
                    for _ in range(ra):
                        if not da:
                            try:
                                next(ga)
                            except StopIteration:
                                da = True

            side = None
            for i in range(NH):
                front(i, side)
                merge(chain_gen(i), cb_gen(i), CHAIN_RATIO)
                side = norm_gen(i)
            for _ in side:
                pass
            if sb == 1:
                stg = g1[0:2, 0:1024]
                for g in range(2):
                    b = ring()
                    for j in range(4):
                        cb = g * 4 + j
                        op('pe', 'transpose', dict(out=psb[b][0:2, j * 128:(j + 1) * 128], in_=ncst[:, cb, :], identity=identf[:]),
                           r=['ncst', 'identf'], w=[PB(b)])
                    op('dve', 'tensor_copy', dict(out=stg[:, g * 512:(g + 1) * 512], in_=psb[b][0:2, 0:512]),
                       r=['ARENA'], w=[PB(b), 't1'])
                op('sp', 'dma_start', dict(out=ncp[l], in_=stg), r=['t1', 'ARENA'], dsem='io0')
                stg2 = g3[0:NSEQ * 2, 0:1024]
                for g in range(2):
                    b = ring()
                    for j in range(4):
                        cb = g * 4 + j
                        op('pe', 'transpose', dict(out=psb[b][0:NSEQ * 2, j * 128:(j + 1) * 128], in_=ncsS[:, cb, :, :].rearrange("p j r -> p (j r)"), identity=identf[:]),
                           r=['ncsS', 'identf'], w=[PB(b)])
                    op('dve', 'tensor_copy', dict(out=stg2[:, g * 512:(g + 1) * 512], in_=psb[b][0:NSEQ * 2, 0:512]),
                       r=['ARENA'], w=[PB(b), 't3'])
                op('sp', 'dma_start', dict(out=ncs[l], in_=stg2), r=['t3', 'ARENA'], dsem='io3')

            mo_p = arena[:, W_HN:W_HN + DC * PMAX].rearrange("p (c t) -> p c t", t=PMAX)
            MO_TOK = ['t1', 't2', 't3', 'G'] + [('bfb', i_) for i_ in range(5)]
            pend_ones = []

            def flush_ones(keep):
                while len(pend_ones) > keep:
                    kw = pend_ones.pop(0)
                    r_ = kw.pop('_r')
                    w_ = kw.pop('_w')
                    op('pe', 'matmul', kw, r=r_, w=w_)

            side = None
            for pi, (c0, c1) in enumerate(pieces):
                N = c1 - c0
                SQ = (RA, RP)[pi % 2]
                for dd in range(DC):
                    slot = wnext()
                    b = ring()
                    for ec in range(DC):
                        if ec == DC // 2:
                            flush_ones(0)
                        op('pe', 'matmul', dict(out=psb[b][:, 0:N], lhsT=wsl[slot][:, ec, :], rhs=mx[:, ec, c0:c1], start=(ec == 0), stop=(ec == DC - 1)),
                           r=[('w', slot, (ec // 4) * 4), ('mx', ec)], w=[PB(b)])
                    if side is not None:
                        next(side, None)
                    sq = sqw[dd % 2][:, 0:N]
                    sqk = ('sqw', dd % 2)
                    op('act', 'activation', dict(out=sq, in_=psb[b][:, 0:N], func=AF.Square),
                       w=[PB(b), sqk])
                    op('dve', 'tensor_copy', dict(out=mo_p[:, dd, 0:N], in_=psb[b][:, 0:N]),
                       w=[PB(b), ('mo', dd)] + (MO_TOK if (pi == 0 and dd == 0) else []))
                    pend_ones.append(dict(out=psb[SQ][:, 0:N], lhsT=onesb[:], rhs=sq, start=(dd == 0), stop=(dd == DC - 1), _r=[sqk, 'onesb'], _w=[PB(SQ)]))
                flush_ones(0)
                if side is not None:
                    for _ in side:
                        pass
                    side = None
                op('dve', 'tensor_scalar', dict(out=oTb[:, c0:c1], in0=psb[SQ][:, 0:N], scalar1=1.0 / D, scalar2=EPS, op0=ALU.mult, op1=ALU.add),
                   w=[PB(SQ), 'oT'])
                op('act', 'activation', dict(out=oTb[:, c0:c1], in_=oTb[:, c0:c1], func=AF.Ln), w=['oT'])
                op('act', 'activation', dict(out=oTb[:, c0:c1], in_=oTb[:, c0:c1], func=AF.Exp, scale=-0.5), w=['oT'])
                for dd in range(DC):
                    op('dve', 'scalar_tensor_tensor', dict(out=mo_p[:, dd, 0:N], in0=mo_p[:, dd, 0:N], scalar=postT[:, l * 16 + dd:l * 16 + dd + 1], in1=oTb[:, c0:c1], op0=ALU.mult, op1=ALU.mult),
                       r=['oT', 'postT'], w=[('mo', dd)])
                    op('pool', 'tensor_tensor', dict(out=hT[:, dd, c0:c1], in0=hT[:, dd, c0:c1], in1=mo_p[:, dd, 0:N], op=ALU.add),
                       r=[('mo', dd)], w=[('hT', dd)])
                if l + 1 < NL:
                    side = prenorm_gen(l + 1, pi)
            if side is not None:
                for _ in side:
                    pass
            for dd in range(DC):
                P.tok_r.setdefault('t1', [])
            for t_ in MO_TOK:
                lst = P.tok_r.setdefault(t_, [])
                for dd in range(DC):
                    if ('mo', dd) in P.tok_w:
                        lst.append(P.tok_w[('mo', dd)])
                    lst.extend(P.tok_r.get(('mo', dd), []))

        P.barrier()
        sti = 0
        for (src, dst, ntok, col0) in iot:
            if dst is None:
                continue
            for q in range(4):
                b = ring()
                for j in range(4):
                    d = q * 4 + j
                    op('pe', 'transpose', dict(out=psb[b][0:ntok, j * 128:(j + 1) * 128], in_=hT[:, d, col0:col0 + ntok], identity=identf[:]),
                       r=[('hT', d), 'identf'], w=[PB(b)])
                stg_full = (g2, g3)[sti % 2]
                key = ('t2', 't3')[sti % 2]
                stg = stg_full[0:ntok, 0:512]
                if sti % 2 == 0:
                    op('act', 'activation', dict(out=stg, in_=psb[b][0:ntok, 0:512], func=AF.Copy),
                       r=['ARENA'], w=[PB(b), key])
                else:
                    op('dve', 'tensor_copy', dict(out=stg, in_=psb[b][0:ntok, 0:512]),
                       r=['ARENA'], w=[PB(b), key])
                op('sp', 'dma_start', dict(out=dst[:, q * 512:(q + 1) * 512], in_=stg),
                   r=[key, 'ARENA'], dsem='io%d' % (1 + sti % 2))
                sti += 1

    P.emit()
    st.close()
    return nc


_NC_CACHE = {}


def kernel(x_prompt, x_sample, state_hgrn, state_conv, meta_tokens, w_in, conv_w, lb_param,
           hgrn_norm_w, w_out, pre_norm_w, post_norm_w):
    NL = w_in.shape[0]
    B, SEQ, _ = x_prompt.shape
    DECB = x_sample.shape[0]
    NSEQ = DECB // N_CORES
    NPCH = SEQ // 128
    key = (NL, NPCH, NSEQ)
    if key not in _NC_CACHE:
        _NC_CACHE[key] = build_nc(NL=NL, NPCH=NPCH, NSEQ=NSEQ)
    nc = _NC_CACHE[key]
    f = lambda a: np.ascontiguousarray(np.asarray(a, dtype=np.float32))
    common = dict(
        meta=f(meta_tokens), w_in=f(w_in), w_out=f(w_out),
        conv_w=f(conv_w).reshape(NL * 24, 128), lb_param=f(lb_param).reshape(NL * 8, 128),
        hnw=f(hgrn_norm_w), prew=f(pre_norm_w).reshape(NL * 16, 128), postw=f(post_norm_w).reshape(NL * 16, 128))
    xs_all = f(x_sample)
    sh_all = f(state_hgrn)
    sc_all = f(state_conv)
    xp_all = f(x_prompt)
    in_maps = []
    for c in range(N_CORES):
        j0 = c * NSEQ
        m = dict(common)
        m['xp'] = xp_all[c % B]
        m['xs'] = np.ascontiguousarray(xs_all[j0:j0 + NSEQ].reshape(NSEQ * 4, D))
        m['sh'] = np.ascontiguousarray(sh_all[:, j0:j0 + NSEQ])
        m['sc'] = np.ascontiguousarray(sc_all[:, j0:j0 + NSEQ].reshape(NL, NSEQ * 2, 1024))
        in_maps.append(m)
    res = run_bass_kernel_spmd(nc, in_maps, core_ids=list(range(N_CORES)))
    R = res.results
    y_prompt = np.stack([R[b]['yp'] for b in range(B)]).astype(np.float32)
    y_sample = np.concatenate([R[c]['ys'].reshape(NSEQ, 4, D) for c in range(N_CORES)], axis=0).astype(np.float32)
    nhp = np.stack([R[b]['nshp'] for b in range(B)], axis=1).astype(np.float32)
    ncp_ = np.stack([R[b]['ncp'] for b in range(B)], axis=1).astype(np.float32)
    nhs = np.concatenate([R[c]['nshs'] for c in range(N_CORES)], axis=1).astype(np.float32)
    ncs_ = np.concatenate([R[c]['ncs'].reshape(NL, NSEQ, 2, 1024) for c in range(N_CORES)], axis=1).astype(np.float32)
    return (y_prompt, y_sample, nhp, ncp_, nhs, ncs_)
```
